# Optimizing a Trainium2 kernel written in Bass

```python
import jax, jax.numpy as jnp
from jax import lax
import numpy as np

D_MODEL = 1024
BATCH = 4
SEQ = 8192
DEPTH = 2

N_HEADS = 8
N_KV_HEADS = 2
HEAD_DIM = 64
ATTN_WIDTH = N_HEADS * HEAD_DIM
KV_WIDTH = N_KV_HEADS * HEAD_DIM
WINDOW = 128
ROT_DIM = HEAD_DIM // 4
ROPE_THETA = 500000.0
HGRN_HEADS = 4
HGRN_HEAD_DIM = 128
HGRN_WIDTH = HGRN_HEADS * HGRN_HEAD_DIM
CHUNK = 64
NORM_EPS = 1e-6
MASK_VALUE = -1e30
SPLITS = (ATTN_WIDTH, KV_WIDTH, KV_WIDTH, ATTN_WIDTH,
          HGRN_WIDTH, HGRN_WIDTH, HGRN_WIDTH, HGRN_WIDTH,
          D_MODEL, D_MODEL)
IN_WIDTH = 2 * ATTN_WIDTH + 2 * KV_WIDTH + 4 * HGRN_WIDTH + 2 * D_MODEL

kernel_name = 'hybrid_swa_sink_hgrn2_gated_merge'


def rms_norm(x, w):
    xf = x.astype(jnp.float32)
    y = xf * lax.rsqrt(jnp.mean(xf * xf, axis=-1, keepdims=True) + NORM_EPS)
    return (y * w.astype(jnp.float32)).astype(x.dtype)


def partial_rope(x, positions):
    half = ROT_DIM // 2
    inv_freq = jnp.power(ROPE_THETA, -jnp.arange(half, dtype=jnp.float32) * (2.0 / ROT_DIM))
    ang = positions.astype(jnp.float32)[..., None] * inv_freq
    cos = jnp.cos(ang)[:, :, None, :]
    sin = jnp.sin(ang)[:, :, None, :]
    xr = x[..., :ROT_DIM].astype(jnp.float32)
    x1, x2 = xr[..., :half], xr[..., half:]
    rot = jnp.concatenate([x1 * cos - x2 * sin, x2 * cos + x1 * sin], axis=-1)
    return jnp.concatenate([rot.astype(x.dtype), x[..., ROT_DIM:]], axis=-1)


def sliding_window_attention(q, k, v, sinks):
    B, T = q.shape[0], q.shape[1]
    nb = T // WINDOW
    G = N_HEADS // N_KV_HEADS
    qb = q.astype(jnp.float32).reshape(B, nb, WINDOW, N_KV_HEADS, G, HEAD_DIM) * (HEAD_DIM ** -0.5)

    def band(a):
        a = a.astype(jnp.float32).reshape(B, nb, WINDOW, N_KV_HEADS, HEAD_DIM)
        prev = jnp.pad(a[:, :-1], ((0, 0), (1, 0), (0, 0), (0, 0), (0, 0)))
        return jnp.concatenate([prev, a], axis=2)

    kb, vb = band(k), band(v)
    blk = jnp.arange(nb)[:, None]
    qpos = blk * WINDOW + jnp.arange(WINDOW)[None, :]
    kpos = (blk - 1) * WINDOW + jnp.arange(2 * WINDOW)[None, :]
    delta = qpos[:, :, None] - kpos[:, None, :]
    mask = (delta >= 0) & (delta < WINDOW) & (kpos[:, None, :] >= 0)
    s = jnp.einsum('bnqhgd,bnkhd->bhgnqk', qb, kb)
    s = jnp.where(mask, s, MASK_VALUE)
    sink = sinks.astype(jnp.float32).reshape(N_KV_HEADS, G)[None, :, :, None, None, None]
    m = jnp.maximum(jnp.max(s, axis=-1, keepdims=True), sink)
    p = jnp.exp(s - m)
    p = p / (jnp.sum(p, axis=-1, keepdims=True) + jnp.exp(sink - m))
    o = jnp.einsum('bhgnqk,bnkhd->bnqhgd', p, vb)
    return o.reshape(B, T, ATTN_WIDTH)


def hgrn2_recurrence(q, k, v, logf):
    B, T, H, Dk = q.shape
    Dv = v.shape[-1]
    n = T // CHUNK

    def chunks(a):
        return a.reshape(B, n, CHUNK, H, a.shape[-1]).transpose(1, 0, 3, 2, 4)

    causal = jnp.tril(jnp.ones((CHUNK, CHUNK), dtype=bool))[:, :, None]

    def step(S, blk):
        qc, kc, vc, gc = blk
        G = jnp.cumsum(gc, axis=-2)
        diff = G[:, :, :, None, :] - G[:, :, None, :, :]
        decay = jnp.exp(jnp.where(causal, diff, MASK_VALUE))
        scores = jnp.einsum('bhtd,bhsd,bhtsd->bhts', qc, kc, decay)
        o = (jnp.einsum('bhts,bhsv->bhtv', scores, vc)
             + jnp.einsum('bhtd,bhdv->bhtv', qc * jnp.exp(G), S))
        G_end = G[:, :, -1:, :]
        S = (S * jnp.exp(G_end)[:, :, 0, :, None]
             + jnp.einsum('bhsd,bhsv->bhdv', kc * jnp.exp(G_end - G), vc))
        return S, o

    S0 = jnp.zeros((B, H, Dk, Dv), jnp.float32)
    _, o = lax.scan(step, S0, (chunks(q), chunks(k), chunks(v), chunks(logf)))
    return o.transpose(1, 0, 3, 2, 4).reshape(B, T, H, Dv)


def setup_inputs(seed: int = 0) -> dict:
    key = jax.random.key(seed)
    ks = jax.random.split(key, 12)
    f32 = jnp.float32
    x = jax.random.normal(ks[0], (BATCH, SEQ, D_MODEL), f32)
    offsets = jax.random.randint(ks[1], (BATCH, 1), 0, 1024, dtype=jnp.int32)
    positions = (jnp.arange(SEQ, dtype=jnp.int32)[None, :] + offsets).astype(jnp.int32)
    norm_w = 1.0 + 0.02 * jax.random.normal(ks[2], (DEPTH, D_MODEL), f32)
    w_in = jax.random.normal(ks[3], (DEPTH, D_MODEL, IN_WIDTH), f32) * D_MODEL ** -0.5
    attn_sinks = 0.5 * jax.random.normal(ks[4], (DEPTH, N_HEADS), f32)
    hgrn_norm_w = 1.0 + 0.02 * jax.random.normal(ks[5], (DEPTH, HGRN_HEAD_DIM), f32)
    w_up_attn = jax.random.normal(ks[6], (DEPTH, ATTN_WIDTH, D_MODEL), f32) * ATTN_WIDTH ** -0.5
    w_up_hgrn = jax.random.normal(ks[7], (DEPTH, HGRN_WIDTH, D_MODEL), f32) * HGRN_WIDTH ** -0.5
    w_out = jax.random.normal(ks[8], (DEPTH, D_MODEL, D_MODEL), f32) * D_MODEL ** -0.5
    lb_logits = 0.1 * jax.random.normal(ks[9], (DEPTH, HGRN_WIDTH), f32)
    final_norm_w = 1.0 + 0.02 * jax.random.normal(ks[10], (D_MODEL,), f32)
    return {'x': x, 'positions': positions, 'norm_w': norm_w, 'w_in': w_in,
            'attn_sinks': attn_sinks, 'hgrn_norm_w': hgrn_norm_w, 'w_up_attn': w_up_attn,
            'w_up_hgrn': w_up_hgrn, 'w_out': w_out, 'lb_logits': lb_logits,
            'final_norm_w': final_norm_w}


def reference(x, positions, norm_w, w_in, attn_sinks, hgrn_norm_w, w_up_attn, w_up_hgrn,
              w_out, lb_logits, final_norm_w):
    B, T, _ = x.shape
    split_points = [int(s) for s in np.cumsum(SPLITS)[:-1]]
    lb_soft = jax.nn.softmax(lb_logits.astype(jnp.float32), axis=0)
    lb_all = jnp.cumsum(lb_soft, axis=0) - lb_soft[0]

    for layer in range(DEPTH):
        h = rms_norm(x, norm_w[layer])
        proj = h @ w_in[layer]
        (q, k, v, z_attn, q_h, f_h, i_h, g_h, gate_attn, gate_hgrn) = jnp.split(proj, split_points, axis=-1)

        q = partial_rope(q.reshape(B, T, N_HEADS, HEAD_DIM), positions)
        k = partial_rope(k.reshape(B, T, N_KV_HEADS, HEAD_DIM), positions)
        v = v.reshape(B, T, N_KV_HEADS, HEAD_DIM)
        a = sliding_window_attention(q, k, v, attn_sinks[layer]).astype(x.dtype) * jax.nn.silu(z_attn)

        lb = lb_all[layer]
        fx = f_h.astype(jnp.float32)
        logf = jnp.log(lb + (1.0 - lb) * jax.nn.sigmoid(fx))
        kin = (1.0 - lb) * jax.nn.sigmoid(-fx)
        qin = jax.nn.silu(q_h.astype(jnp.float32))
        heads = lambda t: t.reshape(B, T, HGRN_HEADS, HGRN_HEAD_DIM)
        o = hgrn2_recurrence(heads(qin), heads(kin), heads(i_h.astype(jnp.float32)), heads(logf))
        o = o * lax.rsqrt(jnp.mean(o * o, axis=-1, keepdims=True) + NORM_EPS) * hgrn_norm_w[layer].astype(jnp.float32)
        b = o.reshape(B, T, HGRN_WIDTH).astype(x.dtype) * jax.nn.silu(g_h)

        merged = (jax.nn.sigmoid(gate_attn) * (a @ w_up_attn[layer])
                  + jax.nn.sigmoid(gate_hgrn) * (b @ w_up_hgrn[layer]))
        x = x + merged @ w_out[layer]

    return rms_norm(x, final_norm_w)
```

```python
import os
import numpy as np
from contextlib import ExitStack
import concourse.bass as bass
import concourse.mybir as mybir
from concourse.bass_utils import run_bass_kernel_spmd

F32 = mybir.dt.float32
BF16 = mybir.dt.bfloat16
I32 = mybir.dt.int32
AF = mybir.ActivationFunctionType
ALU = mybir.AluOpType

D = 1024
SEQ = 8192
BATCH = 4
DEPTH = 2
INW = 5376
EPS = 1e-6
PI = float(np.pi)
NCONST = 128 * 6 + 8 + 4

SAME_ENGINE_SYNC = os.environ.get("K_SES", "1") == "1"

STAGE = float(os.environ.get('K_STAGE', '9'))


class _Op:
    __slots__ = ("eng", "fn", "dma", "deps", "idx", "sig", "signo", "dslot", "dval", "waits")


class Sched:
    ENGS = ("pe", "act", "dve", "pool", "sp")

    def __init__(self, nd=24, same_sync=True):
        self.ops = []
        self.lw = {}
        self.rd = {}
        self.nd = nd
        self.ndma = 0
        self.same_sync = same_sync

    def add(self, eng, fn, reads=(), writes=(), dma=False):
        o = _Op()
        o.eng, o.fn, o.dma, o.sig, o.signo = eng, fn, dma, False, None
        o.idx = len(self.ops)
        deps = {}
        for k in reads:
            p = self.lw.get(k)
            if p is not None:
                deps[p.idx] = p
        for k in writes:
            p = self.lw.get(k)
            if p is not None:
                deps[p.idx] = p
            r = self.rd.get(k)
            if r:
                for p in r.values():
                    deps[p.idx] = p
        o.deps = list(deps.values())
        for k in reads:
            r = self.rd.setdefault(k, {})
            r[("dma", o.idx) if dma else eng] = o
        for k in writes:
            self.lw[k] = o
            self.rd[k] = {}
        if dma:
            o.dslot = self.ndma % self.nd
            o.dval = 16 * (self.ndma // self.nd + 1)
            self.ndma += 1
        self.ops.append(o)
        return o

    def finalize(self):
        for o in self.ops:
            o.waits = []
            for p in o.deps:
                if p.dma:
                    o.waits.append(p)
                elif p.eng == o.eng and (o.eng == "pe" or not self.same_sync):
                    continue
                else:
                    p.sig = True
                    o.waits.append(p)
        cnt = {e: 0 for e in self.ENGS}
        for o in self.ops:
            if (not o.dma) and o.sig:
                cnt[o.eng] += 1
                o.signo = cnt[o.eng]

    def emit(self, eng, engobj, esem, dsem):
        waited = {}
        for o in self.ops:
            if o.eng != eng:
                continue
            for p in o.waits:
                if p.dma:
                    key, val, sem = ("d", p.dslot), p.dval, dsem[p.dslot]
                else:
                    key, val, sem = ("e", p.eng), p.signo, esem[p.eng]
                if waited.get(key, 0) >= val:
                    continue
                waited[key] = val
                engobj.wait_ge(sem, val)
            if o.dma:
                key = ("d", o.dslot)
                if o.dval > 16 and waited.get(key, 0) < o.dval - 16:
                    engobj.wait_ge(dsem[o.dslot], o.dval - 16)
                    waited[key] = o.dval - 16
                ins = o.fn(engobj)
                ins.then_inc(dsem[o.dslot], 16)
            else:
                ins = o.fn(engobj)
                if o.sig:
                    ins.then_inc(esem[o.eng], 1)


def _consts():
    c = np.zeros((128, NCONST), np.float32)
    r = np.arange(128)
    c[:, 0:128] = np.eye(128, dtype=np.float32)
    kk, qq = r[:, None], r[None, :]
    c[:, 128:256] = (kk > qq)
    c[:, 256:384] = (kk <= qq)
    hm = np.zeros((128, 128), np.float32)
    hm[:64, :64] = (kk[:64] <= qq[:, :64])
    hm[:64, 64:] = 1.0
    hm[64:, 64:] = (kk[64:] <= qq[:, 64:])
    c[:, 384:512] = hm
    same = (kk // 64) == (qq // 64)
    a1 = same * ((kk <= qq).astype(np.float32) - ((kk % 64) <= 31).astype(np.float32))
    c[:, 512:640] = a1
    c[:, 640:768] = (kk > qq)
    inv_freq = np.power(np.float32(500000.0),
                        -np.arange(8, dtype=np.float32) * np.float32(2.0 / 16)).astype(np.float32)
    c[:, 768:776] = inv_freq[None, :]
    c[:, 776] = (r <= 31)
    c[:, 777] = (r <= 95)
    c[:, 778] = (r >= 32) & (r < 96)
    c[:, 779] = 1.0
    return c


def build_nc(NT, passes, dbg_names=()):
    NTOK = NT * 128
    nc = bass.Bass("TRN2", target_bir_lowering=False)
    S = Sched(same_sync=SAME_ENGINE_SYNC)
    es = ExitStack()

    def dram(name, shape, dt, kind):
        return nc.dram_tensor(name, list(shape), dt, kind=kind).ap()

    xin = dram("xin", [NTOK, D], F32, "ExternalInput")
    pos_d = dram("pos", [128, NT], I32, "ExternalInput")
    consts_d = dram("consts", [128, NCONST], F32, "ExternalInput")
    lbl_d = dram("lbl", [128, 2 * 512], F32, "ExternalInput")
    fnw_d = dram("fnw", [128, D], F32, "ExternalInput")
    wd = {}
    for p in passes:
        sfx = p["wsuf"]
        if sfx in wd:
            continue
        wd[sfx] = dict(
            w_in=dram("w_in" + sfx, [128, 8 * INW], F32, "ExternalInput"),
            w_ua=dram("w_ua" + sfx, [128, 4 * D], F32, "ExternalInput"),
            w_uh=dram("w_uh" + sfx, [128, 4 * D], F32, "ExternalInput"),
            w_out=dram("w_out" + sfx, [128, 8 * D], F32, "ExternalInput"),
            small=dram("small" + sfx, [128, 8 + 1 + 8 + 2], F32, "ExternalInput"),
        )
    outs = {}
    for p in passes:
        for k in ("raw_out", "norm_out"):
            nm = p.get(k)
            if nm and nm not in outs:
                if nm == "x1":
                    outs[nm] = dram(nm, [NTOK, D], F32, "Internal")
                else:
                    outs[nm] = dram(nm, [NTOK, D], F32, "ExternalOutput")
    dbg_d = {}
    for nm, shape in dbg_names:
        dbg_d[nm] = dram("dbg_" + nm, shape, F32, "ExternalOutput")

    def sb(name, shape, dt):
        return es.enter_context(nc.sbuf_tensor("s_" + name, list(shape), dt))

    def ps(name, shape, dt):
        return es.enter_context(nc.psum_tensor("p_" + name, list(shape), dt))

    W_in = sb("W_in", [128, 8, INW], BF16)
    W_ua = sb("W_ua", [128, 4, D], BF16)
    W_uh = sb("W_uh", [128, 4, D], BF16)
    W_out = sb("W_out", [128, 8, D], BF16)
    cst = sb("cst", [128, NCONST], F32)
    ident = sb("ident", [128, 128], BF16)
    MK = sb("MK", [128, 2, 128], BF16)
    small = sb("small", [128, 19], F32)
    fnw = sb("fnw", [128, D], F32)
    C0 = sb("C0", [128, 512], F32)
    C1 = sb("C1", [128, 512], F32)
    esink = sb("esink", [128, 8], F32)
    mhalf = sb("mhalf", [128, 8], F32)
    posi = sb("posi", [128, NT], I32)
    posf = sb("posf", [128, NT], F32)
    COS = sb("COS", [128, NT, 8], F32)
    SIN = sb("SIN", [128, NT, 8], F32)

    xs = [sb("xs%d" % i, [128, D], F32) for i in range(2)]
    ss = sb("ss", [128, 4], F32)
    rstd = sb("rstd", [128, 4], F32)
    hb = sb("hb", [128, D], BF16)
    hT = sb("hT", [128, 8, 128], BF16)
    qkf = sb("qkf", [128, 10, 64], F32)
    rt = [sb("rt%d" % i, [128, 10, 8], F32) for i in range(4)]
    qkb = sb("qkb", [128, 12, 64], BF16)
    qT = sb("qT", [128, 4, 128], BF16)
    kT = [sb("kT%d" % i, [128, 2, 128], BF16) for i in range(2)]
    vext = [sb("vext%d" % i, [128, 2, 65], BF16) for i in range(3)]
    zs = [sb("zs%d" % i, [128, 512], BF16) for i in range(2)]
    qhs = [sb("qhs%d" % i, [128, 512], BF16) for i in range(2)]
    tf = [sb("tf%d" % i, [128, 512], F32) for i in range(2)]
    vh = [sb("vh%d" % i, [128, 512], BF16) for i in range(2)]
    gs = [sb("gs%d" % i, [128, 512], BF16) for i in range(2)]
    ta = sb("ta", [128, 2, D], BF16)
    th = sb("th", [128, 2, D], BF16)
    ta32 = ta[:].rearrange("p a b -> p (a b)").bitcast(F32)
    th32 = th[:].rearrange("p a b -> p (a b)").bitcast(F32)
    pT = [[sb("pT%d%d" % (g, r), [128, 2, 2, 128], BF16) for r in range(2)] for g in range(2)]
    den = sb("den", [128, 8], F32)
    rden = sb("rden", [128, 8], F32)
    ab = sb("ab", [128, 512], BF16)
    aT = sb("aT", [128, 4, 128], BF16)
    logf = sb("logf", [128, 512], F32)
    kin = sb("kin", [128, 512], BF16)
    cexp = sb("cexp", [128, 4, 4], F32)
    qg = sb("qg", [128, 512], BF16)
    kg = sb("kg", [128, 512], BF16)
    kS = sb("kS", [128, 512], BF16)
    qgT = sb("qgT", [128, 4, 128], BF16)
    kgT = sb("kgT", [128, 4, 128], BF16)
    kgT0w = sb("kgT0w", [128, 4, 64], BF16)
    Sst = sb("Sst", [128, 4, 128], F32)
    Sa = sb("Sa", [128, 4, 128], BF16)
    Sb = sb("Sb", [128, 4, 128], BF16)
    scb = sb("scb", [128, 4, 128], BF16)
    ssh = sb("ssh", [128, 4], F32)
    rsh = sb("rsh", [128, 4], F32)
    bf_ = sb("bf_", [128, 512], F32)
    af = bf_
    bb = sb("bb", [128, 512], BF16)
    bT = sb("bT", [128, 4, 128], BF16)
    t1 = sb("t1", [128, D], F32)
    t2 = sb("t2", [128, D], F32)
    mb = hb
    mT = sb("mT", [128, 8, 128], BF16)
    assert NT * 8 <= 512
    lbl = ta32.rearrange("p (a b) -> p a b", a=2)
    WS = [t1, t2]
    E1v, E2v, E4v, uv = t1[:, 0:512], t1[:, 512:1024], t2[:, 0:512], t2[:, 512:1024]
    tA = xs[0][:, 0:NT * 8].rearrange("p (t j) -> p t j", j=8)
    tB = xs[1][:, 0:NT * 8].rearrange("p (t j) -> p t j", j=8)
    tC = th32[:, 0:NT * 8].rearrange("p (t j) -> p t j", j=8)
    tI = t1[:, 0:NT * 8].bitcast(I32).rearrange("p (t j) -> p t j", j=8)

    Fb = [ps("F%d" % i, [128, 512], F32) for i in range(6)]
    Tb = [ps("T%d" % i, [128, 8, 128], BF16) for i in range(2)]

    c_ident = cst[:, 0:128]
    c_maskP = cst[:, 128:256]
    c_maskC = cst[:, 256:384]
    c_hm = cst[:, 384:512]
    c_a1 = cst[:, 512:640]
    c_a4 = cst[:, 640:768]
    c_invf = cst[:, 768:776]
    c_ind = cst[:, 776:780]

    tr_ctr = [0]

    def next_T():
        t = tr_ctr[0] % int(os.environ.get("K_NT", "2"))
        tr_ctr[0] += 1
        return t

    def dbg(name, key, ap_fn):
        if name in dbg_d:
            S.add("sp", lambda e: e.dma_start(out=dbg_d[name], in_=ap_fn()), reads=[key],
                  writes=[("dbg", name)], dma=True)

    S.add("sp", lambda e: e.dma_start(out=cst[:], in_=consts_d), writes=["cst"], dma=True)
    S.add("sp", lambda e: e.dma_start(out=posi[:], in_=pos_d), writes=["posi"], dma=True)
    S.add("sp", lambda e: e.dma_start(out=fnw[:], in_=fnw_d), writes=["fnw"], dma=True)
    S.add("dve", lambda e: e.tensor_copy(out=ident[:], in_=c_ident), reads=["cst"], writes=["ident"])
    S.add("dve", lambda e: e.tensor_copy(out=MK[:, 0, :], in_=c_maskP), reads=["cst"], writes=["MK0"])
    S.add("dve", lambda e: e.tensor_copy(out=MK[:, 1, :], in_=c_maskC), reads=["cst", "MK0"], writes=["MK"])
    S.add("dve", lambda e: e.memset(mhalf[:], -0.5), writes=["mhalf"])
    S.add("dve", lambda e: e.memset(scb[:], 0.0), writes=["scb"])
    for sl in range(3):
        S.add("dve", lambda e, sl=sl: e.memset(vext[sl][:], 1.0), writes=[("vext", sl)])

    S.add("dve", lambda e: e.tensor_copy(out=posf[:], in_=posi[:]), reads=["posi"], writes=["posf"])
    S.add("dve", lambda e: e.tensor_tensor(
        out=tA, in0=posf[:].unsqueeze(2).to_broadcast([128, NT, 8]),
        in1=c_invf.unsqueeze(1).to_broadcast([128, NT, 8]), op=ALU.mult),
        reads=["posf", "cst"], writes=[("xs", 0)])

    def range_reduce(src_key, shift, dst, dst_key):
        S.add("dve", lambda e: e.tensor_scalar(out=tB, in0=tA, scalar1=float(shift), scalar2=None, op0=ALU.add),
              reads=[src_key], writes=[("xs", 1)])
        S.add("dve", lambda e: e.tensor_scalar(out=tI, in0=tB, scalar1=float(1.0 / (2 * PI)), scalar2=None,
                                                op0=ALU.mult), reads=[("xs", 1)], writes=["E1"])
        S.add("dve", lambda e: e.tensor_copy(out=tC, in_=tI), reads=["E1"], writes=["thall"])
        S.add("dve", lambda e: e.scalar_tensor_tensor(out=tB, in0=tC, scalar=float(-2 * PI), in1=tB,
                                                       op0=ALU.mult, op1=ALU.add), reads=["thall", ("xs", 1)], writes=[("xs", 1)])
        S.add("dve", lambda e: e.tensor_single_scalar(out=tC, in_=tB, scalar=PI, op=ALU.is_gt),
              reads=[("xs", 1)], writes=["thall"])
        S.add("dve", lambda e: e.scalar_tensor_tensor(out=tB, in0=tC, scalar=float(-2 * PI), in1=tB,
                                                       op0=ALU.mult, op1=ALU.add), reads=["thall", ("xs", 1)], writes=[("xs", 1)])
        S.add("dve", lambda e: e.tensor_single_scalar(out=tC, in_=tB, scalar=-PI, op=ALU.is_lt),
              reads=[("xs", 1)], writes=["thall"])
        S.add("dve", lambda e: e.scalar_tensor_tensor(out=tB, in0=tC, scalar=float(2 * PI), in1=tB,
                                                       op0=ALU.mult, op1=ALU.add), reads=["thall", ("xs", 1)], writes=[("xs", 1)])
        S.add("dve", lambda e: e.tensor_scalar(out=tB, in0=tB, scalar1=3.1415925, scalar2=-3.1415925,
                                                op0=ALU.min, op1=ALU.max), reads=[("xs", 1)], writes=[("xs", 1)])
        S.add("act", lambda e: e.activation(out=dst[:], in_=tB, func=AF.Sin), reads=[("xs", 1)], writes=[dst_key])

    range_reduce(("xs", 0), 0.0, SIN, "SIN")
    range_reduce(("xs", 0), PI / 2, COS, "COS")

    def do_pass(pi_, P):
        wsrc = wd[P["wsuf"]]
        src = xin if P["src"] == "xin" else outs[P["src"]]
        src_name = P["src"]
        raw_out = outs.get(P.get("raw_out")) if P.get("raw_out") else None
        raw_name = P.get("raw_out")
        norm_out = outs.get(P.get("norm_out")) if P.get("norm_out") else None
        norm_name = P.get("norm_out")

        S.add("sp", lambda e: e.dma_start(out=small[:], in_=wsrc["small"]), writes=["small"], dma=True)
        nwv = small[:, 0:8]
        hnw = small[:, 8:9]
        sinks = small[:, 9:17]
        lsel = small[:, 17:19]
        S.add("act", lambda e: e.activation(out=esink[:], in_=sinks, func=AF.Exp), reads=["small"], writes=["esink"])
        LK = [("ta", pp, off) for pp in range(2) for off in (0, 512)]
        S.add("sp", lambda e: e.dma_start(out=ta32, in_=lbl_d), writes=LK, dma=True)
        S.add("act", lambda e: e.activation(out=ta32, in_=ta32, func=AF.Exp), reads=LK, writes=LK)
        S.add("dve", lambda e: e.tensor_tensor(out=C0[:], in0=lbl[:, 0, :], in1=lbl[:, 1, :], op=ALU.add),
              reads=LK, writes=["C0"])
        S.add("dve", lambda e: e.reciprocal(out=C0[:], in_=C0[:]), reads=["C0"], writes=["C0"])
        S.add("dve", lambda e: e.tensor_scalar(out=C1[:], in0=lbl[:, 0, :], scalar1=lsel[:, 0:1], scalar2=None,
                                                op0=ALU.mult), reads=LK + ["small"], writes=["C1"])
        S.add("dve", lambda e: e.scalar_tensor_tensor(out=C1[:], in0=lbl[:, 1, :], scalar=lsel[:, 1:2], in1=C1[:],
                                                       op0=ALU.mult, op1=ALU.add), reads=LK + ["small", "C1"],
              writes=["C1"])
        S.add("dve", lambda e: e.tensor_tensor(out=C1[:], in0=C1[:], in1=C0[:], op=ALU.mult),
              reads=["C1", "C0"], writes=["C1"])
        S.add("dve", lambda e: e.tensor_scalar(out=C0[:], in0=C1[:], scalar1=0.5, scalar2=0.5, op0=ALU.mult,
                                                op1=ALU.add), reads=["C1"], writes=["C0"])
        S.add("dve", lambda e: e.tensor_scalar(out=C1[:], in0=C1[:], scalar1=-0.5, scalar2=0.5, op0=ALU.mult,
                                                op1=ALU.add), reads=["C1"], writes=["C1"])

        S.add("dve", lambda e: e.memset(ss[:, 2:3], 0.0), writes=["Wdone"])
        pieces = []
        for k in range(8):
            for q4 in range(6):
                wdt_ = 1024 if q4 < 5 else 256
                pieces.append((wsrc["w_in"][:, k * INW + q4 * 1024: k * INW + q4 * 1024 + wdt_],
                               W_in[:, k, q4 * 1024:q4 * 1024 + wdt_], ("nw", k), wdt_))
        for k in range(4):
            pieces.append((wsrc["w_ua"][:, k * D:(k + 1) * D], W_ua[:, k, :], ("half",), 1024))
        for k in range(4):
            pieces.append((wsrc["w_uh"][:, k * D:(k + 1) * D], W_uh[:, k, :], ("hnw",), 1024))
        for k in range(8):
            pieces.append((wsrc["w_out"][:, k * D:(k + 1) * D], W_out[:, k, :], ("half",), 1024))
        if STAGE < 1:
            pieces = []
        for n, (srcap, dstap, scl, wdt) in enumerate(pieces):
            st = n % 2
            S.add("sp", lambda e, srcap=srcap, st=st, wdt=wdt: e.dma_start(out=WS[st][:, 0:wdt], in_=srcap),
                  writes=[("E1", "E2"), ("E4", "u")][st], dma=True)
            if len(dstap.shape) == 3:
                dstv = dstap.rearrange("p a b -> p (a b)")
            else:
                dstv = dstap
            if scl is None:
                sc = 1.0
            elif scl[0] == "nw":
                sc = nwv[:, scl[1]:scl[1] + 1]
            elif scl[0] == "hnw":
                sc = hnw
            else:
                sc = 0.5
            wkey = ("W", n)
            engs = tuple(os.environ.get("K_WENG", "dve,act,pool").split(","))
            eg = engs[n % len(engs)]
            if eg == "act":
                S.add("act", lambda e, dstv=dstv, st=st, wdt=wdt, sc=sc: e.activation(
                    out=dstv, in_=WS[st][:, 0:wdt], func=AF.Copy, scale=sc),
                    reads=[("E1", "E2"), ("E4", "u")][st] + ("small", "Wdone"), writes=[wkey])
            else:
                S.add(eg, lambda e, dstv=dstv, st=st, wdt=wdt, sc=sc: e.tensor_scalar(
                    out=dstv, in0=WS[st][:, 0:wdt], scalar1=sc, scalar2=None, op0=ALU.mult),
                    reads=[("E1", "E2"), ("E4", "u")][st] + ("small", "Wdone"), writes=[wkey])
        S.add("dve", lambda e: e.memset(ss[:, 3:4], 0.0), reads=[("W", n) for n in range(len(pieces))],
              writes=["Wdone"])
        S.add("dve", lambda e: e.memset(Sst[:], 0.0), writes=["Sst"])

        def do_tile_pt(i):
            x = xs[i % 2]
            rows = slice(i * 128, (i + 1) * 128)
            S.add("sp", lambda e: e.dma_start(out=x[:], in_=src[rows, :]), writes=[("xs", i % 2)], dma=True)
            for nm_, o_ in ((raw_name, raw_out), (norm_name, norm_out)):
                if o_ is not None:
                    S.add("sp", lambda e, o_=o_: e.dma_start(out=o_[rows, :], in_=x[:]),
                          reads=[("xs", i % 2)], writes=[(nm_, i)], dma=True)

        chunks = [(0, 512), (512, 256), (768, 512), (1280, 512), (1792, 512), (2304, 512), (2816, 512),
                  (3328, 512), (3840, 512), (4352, 512), (4864, 512)]

        def stageA(i):
            par = i % 2
            x = xs[par]
            xkey = ("xs", par)
            v3 = i % 3
            rows = slice(i * 128, (i + 1) * 128)
            rd = [(src_name, i)] if src_name != "xin" else []
            S.add("sp", lambda e: e.dma_start(out=x[:], in_=src[rows, :]), reads=rd, writes=[xkey], dma=True)
            S.add("act", lambda e: e.activation(out=hb[:], in_=x[:], func=AF.Square, accum_out=ss[:, 0:1]),
                  reads=[xkey], writes=["hb", "ss0"])
            S.add("pool", lambda e: e.tensor_scalar(out=rstd[:, 0:1], in0=ss[:, 0:1], scalar1=1.0 / D, scalar2=EPS,
                                                     op0=ALU.mult, op1=ALU.add), reads=["ss0"], writes=["rstd0a"])
            S.add("pool", lambda e: e.tensor_tensor(out=rstd[:, 0:1], in0=rstd[:, 0:1], in1=mhalf[:, 0:1], op=ALU.pow),
                  reads=["rstd0a", "mhalf"], writes=["rstd0"])
            S.add("act", lambda e: e.activation(out=hb[:], in_=x[:], func=AF.Copy, scale=rstd[:, 0:1]),
                  reads=[xkey, "rstd0"], writes=["hb"])
            for k in range(8):
                S.add("pe", lambda e, k=k: e.transpose(out=Tb[0][:, k, :], in_=hb[:, k * 128:(k + 1) * 128],
                                                       identity=ident[:]),
                      reads=["hb", "ident"], writes=[("T", 0)])
            S.add("dve", lambda e: e.tensor_copy(out=hT[:], in_=Tb[0][:]), reads=[("T", 0)], writes=["hT"])
            yield
            for c, (c0, cw) in enumerate(chunks):
                bk = c % 2
                for k in range(8):
                    S.add("pe", lambda e, k=k, c0=c0, cw=cw, bk=bk: e.matmul(
                        Fb[bk][:, 0:cw], lhsT=hT[:, k, :], rhs=W_in[:, k, c0:c0 + cw], start=(k == 0), stop=(k == 7)),
                        reads=["hT", "Wdone"], writes=[("F", bk)])
                Fk = ("F", bk)
                if c == 0:
                    S.add("dve", lambda e, bk=bk: e.tensor_copy(out=qkf[:, 0:8, :].rearrange("p a b -> p (a b)"),
                                                                 in_=Fb[bk][:, 0:512]), reads=[Fk], writes=["qkf_q"])
                elif c == 1:
                    S.add("dve", lambda e, bk=bk: e.tensor_copy(out=qkf[:, 8:10, :].rearrange("p a b -> p (a b)"),
                                                                 in_=Fb[bk][:, 0:128]), reads=[Fk], writes=["qkf_k"])
                    S.add("dve", lambda e, bk=bk: e.tensor_copy(
                        out=vext[v3][:, :, 0:64], in_=Fb[bk][:, 128:256].rearrange("p (g d) -> p g d", g=2)),
                        reads=[Fk], writes=[("vext", v3)])
                    cosb = COS[:, i, :].unsqueeze(1).to_broadcast([128, 10, 8])
                    sinb = SIN[:, i, :].unsqueeze(1).to_broadcast([128, 10, 8])
                    x1v, x2v = qkf[:, :, 0:8], qkf[:, :, 8:16]
                    qk_keys = ["qkf_q", "qkf_k"]
                    S.add("pool", lambda e: e.tensor_tensor(out=rt[0][:], in0=x1v, in1=cosb, op=ALU.mult),
                          reads=qk_keys + ["COS"], writes=["rt0"])
                    S.add("pool", lambda e: e.tensor_tensor(out=rt[1][:], in0=x2v, in1=sinb, op=ALU.mult),
                          reads=qk_keys + ["SIN"], writes=["rt1"])
                    S.add("pool", lambda e: e.tensor_tensor(out=rt[2][:], in0=x2v, in1=cosb, op=ALU.mult),
                          reads=qk_keys + ["COS"], writes=["rt2"])
                    S.add("pool", lambda e: e.tensor_tensor(out=rt[3][:], in0=x1v, in1=sinb, op=ALU.mult),
                          reads=qk_keys + ["SIN"], writes=["rt3"])
                    S.add("pool", lambda e: e.tensor_tensor(out=x1v, in0=rt[0][:], in1=rt[1][:], op=ALU.subtract),
                          reads=["rt0", "rt1", "rt2", "rt3"], writes=["qkf_r1"])
                    S.add("pool", lambda e: e.tensor_tensor(out=x2v, in0=rt[2][:], in1=rt[3][:], op=ALU.add),
                          reads=["rt2", "rt3", "qkf_r1"], writes=["qkf_r2"])
                    S.add("dve", lambda e: e.tensor_scalar(out=qkb[:, 0:8, :], in0=qkf[:, 0:8, :], scalar1=0.125, scalar2=None,
                                                            op0=ALU.mult), reads=["qkf_q", "qkf_r1", "qkf_r2"], writes=["qkb_q"])
                    S.add("dve", lambda e: e.tensor_copy(
                        out=qkb[:, 8:12, :].rearrange("p (g r) d -> p g r d", g=2),
                        in_=qkf[:, 8:10, :].unsqueeze(2).to_broadcast([128, 2, 2, 64])),
                        reads=["qkf_k", "qkf_r1", "qkf_r2"], writes=["qkb_k"])
                elif c in (2, 3, 6):
                    dst, dk = {2: (zs, "zs"), 3: (qhs, "qhs"), 6: (gs, "gs")}[c]
                    S.add("act", lambda e, bk=bk, dst=dst: e.activation(out=dst[par][:], in_=Fb[bk][:, 0:512],
                                                                        func=AF.Tanh, scale=0.5),
                          reads=[Fk], writes=[(dk, par)])
                    S.add("dve", lambda e, bk=bk, dst=dst: e.scalar_tensor_tensor(
                        out=dst[par][:], in0=dst[par][:], scalar=1.0, in1=Fb[bk][:, 0:512], op0=ALU.add, op1=ALU.mult),
                        reads=[Fk, (dk, par)], writes=[(dk, par)])
                elif c == 4:
                    S.add("act", lambda e, bk=bk: e.activation(out=tf[par][:], in_=Fb[bk][:, 0:512], func=AF.Tanh,
                                                               scale=0.5), reads=[Fk], writes=[("tf", par)])
                    S.add("dve", lambda e: e.tensor_tensor(out=tf[par][:], in0=tf[par][:], in1=C1[:], op=ALU.mult),
                          reads=[("tf", par), "C1"], writes=[("tf", par)])
                    S.add("dve", lambda e: e.tensor_tensor(out=tf[par][:], in0=tf[par][:], in1=C0[:], op=ALU.add),
                          reads=[("tf", par), "C0"], writes=[("tf", par)])
                    S.add("act", lambda e: e.activation(out=logf[:], in_=tf[par][:], func=AF.Ln),
                          reads=[("tf", par)], writes=["logf"])
                    S.add("dve", lambda e: e.tensor_scalar(out=kin[:], in0=tf[par][:], scalar1=-1.0, scalar2=1.0,
                                                            op0=ALU.mult, op1=ALU.add),
                          reads=[("tf", par)], writes=["kin"])
                elif c == 5:
                    S.add("act", lambda e, bk=bk: e.activation(out=vh[par][:], in_=Fb[bk][:, 0:512], func=AF.Copy),
                          reads=[Fk], writes=[("vh", par)])
                else:
                    dst, dk = (ta, "ta") if c in (7, 8) else (th, "th")
                    off = 0 if c in (7, 9) else 512
                    wk = [(dk, par, off)] + (["thall"] if dk == "th" else [])
                    S.add("act", lambda e, bk=bk, dst=dst, off=off: e.activation(
                        out=dst[:, par, off:off + 512], in_=Fb[bk][:, 0:512], func=AF.Tanh, scale=0.5),
                        reads=[Fk], writes=wk)
                yield

        def stageATT(i):
            first = (i == 0)
            par = i % 2
            x = xs[par]
            xkey = ("xs", par)
            cur, prv = i % 2, (i - 1) % 2
            v3, v3p = i % 3, (i - 1) % 3
            rows = slice(i * 128, (i + 1) * 128)
            T = 1
            for c in range(6):
                S.add("pe", lambda e, c=c: e.transpose(
                    out=Tb[T][:, c, :], in_=qkb[:, 2 * c:2 * c + 2, :].rearrange("p a b -> p (a b)"),
                    identity=ident[:]), reads=["qkb_q", "qkb_k", "ident"], writes=[("T", T)])
            S.add("dve", lambda e: e.tensor_copy(out=qT[:], in_=Tb[T][:, 0:4, :]), reads=[("T", T)], writes=["qT"])
            S.add("dve", lambda e: e.tensor_copy(out=kT[cur][:], in_=Tb[T][:, 4:6, :]),
                  reads=[("T", T)], writes=[("kT", cur)])
            yield
            js = [1] if first else [0, 1]
            for g in range(2):
                for r in range(2):
                    bk = 2 + r
                    for j in js:
                        ksl = prv if j == 0 else cur
                        S.add("pe", lambda e, g=g, r=r, j=j, ksl=ksl, bk=bk: e.matmul(
                            Fb[bk][:, j * 256:(j + 1) * 256],
                            lhsT=kT[ksl][64 * r:64 * r + 64, g, :],
                            rhs=qT[64 * r:64 * r + 64, 2 * g:2 * g + 2, :].rearrange("p a b -> p (a b)"),
                            start=True, stop=True),
                            reads=[("kT", ksl), "qT"], writes=[("F", bk)])
                    lo = js[0] * 256
                    S.add("act", lambda e, g=g, r=r, bk=bk, lo=lo: e.activation(
                        out=pT[g][r][:].rearrange("p a b c -> p (a b c)")[:, lo:512], in_=Fb[bk][:, lo:512],
                        func=AF.Exp), reads=[("F", bk)], writes=[("pT", g, r)])
                    j0 = js[0]
                    S.add("pool", lambda e, g=g, r=r, j0=j0: e.tensor_tensor(
                        out=pT[g][r][:, j0:2, :, :], in0=pT[g][r][:, j0:2, :, :],
                        in1=MK[:, j0:2, :].unsqueeze(2).to_broadcast([128, 2 - j0, 2, 128]), op=ALU.mult),
                        reads=[("pT", g, r), "MK"], writes=[("pT", g, r)])
                yield
            obanks = {0: 2, 1: 3}
            for g in range(2):
                bk = obanks[g]
                for hl in range(4):
                    r, cc = hl % 2, hl // 2
                    for jn, j in enumerate(js):
                        vsl = v3p if j == 0 else v3
                        S.add("pe", lambda e, g=g, r=r, cc=cc, j=j, hl=hl, vsl=vsl, bk=bk, st_=(jn == 0),
                              sp_=(jn == len(js) - 1): e.matmul(
                            Fb[bk][:, hl * 65:(hl + 1) * 65], lhsT=pT[g][r][:, j, cc, :], rhs=vext[vsl][:, g, :],
                            start=st_, stop=sp_),
                            reads=[("pT", g, r), ("vext", vsl)], writes=[("F", bk)])
                ov = Fb[bk][:, 0:260].rearrange("p (h d) -> p h d", d=65)
                S.add("dve", lambda e, g=g, ov=ov: e.tensor_tensor(
                    out=den[:, 4 * g:4 * g + 4], in0=ov[:, :, 64], in1=esink[:, 4 * g:4 * g + 4], op=ALU.add),
                    reads=[("F", bk), "esink"], writes=[("den", g)])
                S.add("dve", lambda e, g=g: e.reciprocal(out=rden[:, 4 * g:4 * g + 4], in_=den[:, 4 * g:4 * g + 4]),
                      reads=[("den", g)], writes=[("rden", g)])
                S.add("dve", lambda e, g=g, ov=ov: e.tensor_tensor(
                    out=af[:, 256 * g:256 * g + 256].rearrange("p (h d) -> p h d", d=64), in0=ov[:, :, 0:64],
                    in1=rden[:, 4 * g:4 * g + 4].unsqueeze(2).to_broadcast([128, 4, 64]), op=ALU.mult),
                    reads=[("F", bk), ("rden", g)], writes=[("af", g)])
            S.add("pool", lambda e: e.tensor_tensor(out=ab[:], in0=af[:], in1=zs[par][:], op=ALU.mult),
                  reads=[("af", 0), ("af", 1), ("zs", par)], writes=["ab"])
            yield
            for c in range(4):
                S.add("pe", lambda e, c=c: e.transpose(out=Tb[T][:, c, :], in_=ab[:, c * 128:(c + 1) * 128],
                                                       identity=ident[:]),
                      reads=["ab", "ident"], writes=[("T", T)])
            S.add("dve", lambda e: e.tensor_copy(out=aT[:], in_=Tb[T][:, 0:4, :]), reads=[("T", T)], writes=["aT"])
            yield

        def stageHG(i):
            first = (i == 0)
            par = i % 2
            x = xs[par]
            xkey = ("xs", par)
            cur, prv = i % 2, (i - 1) % 2
            v3, v3p = i % 3, (i - 1) % 3
            rows = slice(i * 128, (i + 1) * 128)
            T = 0
            S.add("pe", lambda e: e.matmul(Fb[4][:, :], lhsT=c_a1, rhs=logf[:], start=True, stop=True),
                  reads=["logf", "cst"], writes=[("F", 4)])
            S.add("act", lambda e: e.activation(out=E1v, in_=Fb[4][:, :], func=AF.Exp), reads=[("F", 4)], writes=["E1"])
            S.add("act", lambda e: e.activation(out=E2v, in_=Fb[4][:, :], func=AF.Exp, scale=-1.0),
                  reads=[("F", 4)], writes=["E2"])
            S.add("pe", lambda e: e.matmul(Fb[5][:, :], lhsT=c_a4, rhs=logf[:], start=True, stop=True),
                  reads=["logf", "cst"], writes=[("F", 5)])
            S.add("act", lambda e: e.activation(out=E4v, in_=Fb[5][:, :], func=AF.Exp), reads=[("F", 5)], writes=["E4"])
            for h in range(4):
                S.add("pe", lambda e, h=h: e.matmul(Fb[4][:, 4 * h:4 * h + 4], lhsT=logf[:, h * 128:(h + 1) * 128],
                                                    rhs=c_ind, start=True, stop=True),
                      reads=["logf", "cst"], writes=[("F", 4)])
            S.add("act", lambda e: e.activation(out=cexp[:].rearrange("p a b -> p (a b)"), in_=Fb[4][:, 0:16],
                                                func=AF.Exp), reads=[("F", 4)], writes=["cexp"])
            S.add("dve", lambda e: e.tensor_tensor(out=qg[:], in0=qhs[par][:], in1=E1v, op=ALU.mult),
                  reads=[("qhs", par), "E1"], writes=["qg"])
            S.add("pool", lambda e: e.tensor_tensor(out=kg[:], in0=kin[:], in1=E2v, op=ALU.mult),
                  reads=["kin", "E2"], writes=["kg"])
            S.add("pool", lambda e: e.tensor_tensor(out=kS[:], in0=kin[:], in1=E4v, op=ALU.mult),
                  reads=["kin", "E4"], writes=["kS"])
            yield
            for c in range(4):
                S.add("pe", lambda e, c=c: e.transpose(out=Tb[T][:, c, :], in_=qg[:, c * 128:(c + 1) * 128],
                                                       identity=ident[:]),
                      reads=["qg", "ident"], writes=[("T", T)])
            for c in range(4):
                S.add("pe", lambda e, c=c: e.transpose(out=Tb[T][:, 4 + c, :], in_=kg[:, c * 128:(c + 1) * 128],
                                                       identity=ident[:]),
                      reads=["kg", "ident"], writes=[("T", T)])
            S.add("dve", lambda e: e.tensor_copy(out=qgT[:], in_=Tb[T][:, 0:4, :]), reads=[("T", T)], writes=["qgT"])
            S.add("dve", lambda e: e.tensor_copy(out=kgT[:], in_=Tb[T][:, 4:8, :]), reads=[("T", T)], writes=["kgT"])
            S.add("dve", lambda e: e.tensor_tensor(
                out=kgT0w[:], in0=kgT[:, :, 0:64], in1=cexp[:, :, 2:3].to_broadcast([128, 4, 64]), op=ALU.mult),
                reads=["kgT", "cexp"], writes=["kgT0w"])
            if not first:
                S.add("dve", lambda e: e.tensor_tensor(
                    out=Sa[:], in0=Sst[:], in1=cexp[:, :, 0:1].to_broadcast([128, 4, 128]), op=ALU.mult),
                    reads=["Sst", "cexp"], writes=["Sa"])
                S.add("pool", lambda e: e.tensor_tensor(
                    out=Sb[:], in0=Sst[:], in1=cexp[:, :, 1:2].to_broadcast([128, 4, 128]), op=ALU.mult),
                    reads=["Sst", "cexp"], writes=["Sb"])
            yield
            scv = Fb[5][:, :].rearrange("p (h t) -> p h t", h=4)
            for h in range(4):
                S.add("pe", lambda e, h=h: e.matmul(scv[0:64, h, 0:64], lhsT=kgT[:, h, 0:64], rhs=qgT[:, h, 0:64],
                                                    start=True, stop=True),
                      reads=["kgT", "qgT"], writes=[("F", 5)])
                S.add("pe", lambda e, h=h: e.matmul(scv[0:64, h, 64:128], lhsT=kgT0w[:, h, :], rhs=qgT[:, h, 64:128],
                                                    start=True, stop=True),
                      reads=["kgT0w", "qgT"], writes=[("F", 5)])
                S.add("pe", lambda e, h=h: e.matmul(scv[64:128, h, 64:128], lhsT=kgT[:, h, 64:128],
                                                    rhs=qgT[:, h, 64:128], start=True, stop=True),
                      reads=["kgT", "qgT"], writes=[("F", 5)])
            S.add("dve", lambda e: e.tensor_tensor(
                out=scb[0:64, :, :], in0=scv[0:64, :, :],
                in1=c_hm[0:64, :].unsqueeze(1).to_broadcast([64, 4, 128]), op=ALU.mult),
                reads=[("F", 5), "cst", "scb"], writes=["scb_a"])
            S.add("dve", lambda e: e.tensor_tensor(
                out=scb[64:128, :, 64:128], in0=scv[64:128, :, 64:128],
                in1=c_hm[64:128, 64:128].unsqueeze(1).to_broadcast([64, 4, 64]), op=ALU.mult),
                reads=[("F", 5), "cst", "scb"], writes=["scb_b"])
            for h in range(4):
                S.add("pe", lambda e, h=h: e.matmul(Fb[4][:, h * 128:(h + 1) * 128], lhsT=kS[:, h * 128:(h + 1) * 128],
                                                    rhs=vh[par][:, h * 128:(h + 1) * 128], start=True, stop=True),
                      reads=["kS", ("vh", par), "cexp"], writes=[("F", 4)])
            yield
            for h in range(4):
                S.add("pe", lambda e, h=h: e.matmul(Fb[5][:, h * 128:(h + 1) * 128], lhsT=scb[:, h, :],
                                                    rhs=vh[par][:, h * 128:(h + 1) * 128], start=True, stop=first),
                      reads=["scb_a", "scb_b", ("vh", par)], writes=[("F", 5)])
                if not first:
                    S.add("pe", lambda e, h=h: e.matmul(Fb[5][0:64, h * 128:(h + 1) * 128], lhsT=qgT[:, h, 0:64],
                                                        rhs=Sa[:, h, :], start=False, stop=True),
                          reads=["qgT", "Sa"], writes=[("F", 5)])
                    S.add("pe", lambda e, h=h: e.matmul(Fb[5][64:128, h * 128:(h + 1) * 128], lhsT=qgT[:, h, 64:128],
                                                        rhs=Sb[:, h, :], start=False, stop=True),
                          reads=["qgT", "Sb"], writes=[("F", 5)])
            Sv = Sst[:].rearrange("p a b -> p (a b)")
            if first:
                S.add("dve", lambda e: e.tensor_copy(out=Sv, in_=Fb[4][:, :]), reads=[("F", 4), "Sst"], writes=["Sst"])
            else:
                S.add("dve", lambda e: e.tensor_tensor(
                    out=Sst[:], in0=Sst[:], in1=cexp[:, :, 3:4].to_broadcast([128, 4, 128]), op=ALU.mult),
                    reads=["Sst", "cexp", "Sa", "Sb"], writes=["Sst"])
                S.add("dve", lambda e: e.tensor_tensor(out=Sv, in0=Sv, in1=Fb[4][:, :], op=ALU.add),
                      reads=["Sst", ("F", 4)], writes=["Sst"])
            for h in range(4):
                S.add("act", lambda e, h=h: e.activation(out=bb[:, h * 128:(h + 1) * 128],
                                                         in_=Fb[5][:, h * 128:(h + 1) * 128], func=AF.Square,
                                                         accum_out=ssh[:, h:h + 1]),
                      reads=[("F", 5)], writes=["bb", ("ssh", h)])
            S.add("pool", lambda e: e.tensor_scalar(out=rsh[:], in0=ssh[:], scalar1=4.0 / 128, scalar2=16.0 * EPS,
                                                     op0=ALU.mult, op1=ALU.add),
                  reads=[("ssh", h) for h in range(4)], writes=["rsha"])
            S.add("pool", lambda e: e.tensor_tensor(out=rsh[:], in0=rsh[:], in1=mhalf[:, 0:4], op=ALU.pow),
                  reads=["rsha", "mhalf"], writes=["rsh"])
            S.add("dve", lambda e: e.tensor_tensor(
                out=bf_[:].rearrange("p (h d) -> p h d", h=4), in0=Fb[5][:, :].rearrange("p (h d) -> p h d", h=4),
                in1=rsh[:].unsqueeze(2).to_broadcast([128, 4, 128]), op=ALU.mult),
                reads=[("F", 5), "rsh"], writes=[("af", 0), ("af", 1)])
            S.add("pool", lambda e: e.tensor_tensor(out=bb[:], in0=bf_[:], in1=gs[par][:], op=ALU.mult),
                  reads=[("af", 0), ("af", 1), ("gs", par)], writes=["bb"])
            yield
            for c in range(4):
                S.add("pe", lambda e, c=c: e.transpose(out=Tb[T][:, c, :], in_=bb[:, c * 128:(c + 1) * 128],
                                                       identity=ident[:]),
                      reads=["bb", "ident"], writes=[("T", T)])
            S.add("dve", lambda e: e.tensor_copy(out=bT[:], in_=Tb[T][:, 0:4, :]), reads=[("T", T)], writes=["bT"])
            yield

        def stageC(i):
            first = (i == 0)
            par = i % 2
            x = xs[par]
            xkey = ("xs", par)
            cur, prv = i % 2, (i - 1) % 2
            v3, v3p = i % 3, (i - 1) % 3
            rows = slice(i * 128, (i + 1) * 128)
            T = 1
            for n in range(2):
                bk = (2, 3)[n]
                for k in range(4):
                    S.add("pe", lambda e, n=n, k=k, bk=bk: e.matmul(Fb[bk][:, :], lhsT=aT[:, k, :],
                                                                    rhs=W_ua[:, k, n * 512:(n + 1) * 512],
                                                                    start=(k == 0), stop=(k == 3)),
                          reads=["aT", "Wdone"], writes=[("F", bk)])
                S.add("dve", lambda e, n=n, bk=bk: e.scalar_tensor_tensor(
                    out=t1[:, n * 512:(n + 1) * 512], in0=ta[:, par, n * 512:(n + 1) * 512], scalar=1.0,
                    in1=Fb[bk][:, :], op0=ALU.add, op1=ALU.mult),
                    reads=[("ta", par, n * 512), ("F", bk)], writes=[("E1", "E2")[n]])
            yield
            for n in range(2):
                bk = (4, 5)[n]
                for k in range(4):
                    S.add("pe", lambda e, n=n, k=k, bk=bk: e.matmul(Fb[bk][:, :], lhsT=bT[:, k, :],
                                                                    rhs=W_uh[:, k, n * 512:(n + 1) * 512],
                                                                    start=(k == 0), stop=(k == 3)),
                          reads=["bT", "Wdone"], writes=[("F", bk)])
                S.add("dve", lambda e, n=n, bk=bk: e.scalar_tensor_tensor(
                    out=t2[:, n * 512:(n + 1) * 512], in0=th[:, par, n * 512:(n + 1) * 512], scalar=1.0,
                    in1=Fb[bk][:, :], op0=ALU.add, op1=ALU.mult),
                    reads=[("th", par, n * 512), ("F", bk)], writes=[("E4", "u")[n]])
                S.add("pool", lambda e, n=n: e.tensor_tensor(
                    out=mb[:, n * 512:(n + 1) * 512], in0=t1[:, n * 512:(n + 1) * 512],
                    in1=t2[:, n * 512:(n + 1) * 512], op=ALU.add),
                    reads=[("E1", "E2")[n], ("E4", "u")[n]], writes=["hb"])
            yield
            for k in range(8):
                S.add("pe", lambda e, k=k: e.transpose(out=Tb[T][:, k, :], in_=mb[:, k * 128:(k + 1) * 128],
                                                       identity=ident[:]),
                      reads=["hb", "ident"], writes=[("T", T)])
            S.add("dve", lambda e: e.tensor_copy(out=mT[:], in_=Tb[T][:]), reads=[("T", T)], writes=["mT"])
            yield
            for n in range(2):
                bk = (2, 3)[n]
                for k in range(8):
                    S.add("pe", lambda e, n=n, k=k, bk=bk: e.matmul(Fb[bk][:, :], lhsT=mT[:, k, :],
                                                                    rhs=W_out[:, k, n * 512:(n + 1) * 512],
                                                                    start=(k == 0), stop=(k == 7)),
                          reads=["mT", "Wdone"], writes=[("F", bk)])
                S.add("dve", lambda e, n=n, bk=bk: e.tensor_tensor(
                    out=x[:, n * 512:(n + 1) * 512], in0=x[:, n * 512:(n + 1) * 512], in1=Fb[bk][:, :],
                    op=ALU.add), reads=[xkey, ("F", bk)], writes=[xkey])
                yield
            if raw_out is not None:
                S.add("sp", lambda e: e.dma_start(out=raw_out[rows, :], in_=x[:]),
                      reads=[xkey], writes=[(raw_name, i)], dma=True)
            if norm_out is not None:
                S.add("act", lambda e: e.activation(out=bb[:, 0:512], in_=x[:, 0:512], func=AF.Square,
                                                    accum_out=ss[:, 1:2]),
                      reads=[xkey], writes=["bb", "ss1"])
                S.add("act", lambda e: e.activation(out=bb[:, 0:512], in_=x[:, 512:1024], func=AF.Square,
                                                    accum_out=ss[:, 2:3]),
                      reads=[xkey], writes=["bb", "ss2"])
                S.add("pool", lambda e: e.tensor_tensor(out=ss[:, 1:2], in0=ss[:, 1:2], in1=ss[:, 2:3], op=ALU.add),
                      reads=["ss1", "ss2"], writes=["ss1"])
                S.add("pool", lambda e: e.tensor_scalar(out=rstd[:, 1:2], in0=ss[:, 1:2], scalar1=1.0 / D, scalar2=EPS,
                                                         op0=ALU.mult, op1=ALU.add), reads=["ss1"], writes=["rstd1a"])
                S.add("pool", lambda e: e.tensor_tensor(out=rstd[:, 1:2], in0=rstd[:, 1:2], in1=mhalf[:, 0:1],
                                                         op=ALU.pow), reads=["rstd1a", "mhalf"], writes=["rstd1"])
                S.add("dve", lambda e: e.scalar_tensor_tensor(
                    out=t1[:], in0=x[:], scalar=rstd[:, 1:2], in1=fnw[:], op0=ALU.mult, op1=ALU.mult),
                    reads=[xkey, "rstd1", "fnw"], writes=["E1", "E2"])
                S.add("sp", lambda e: e.dma_start(out=norm_out[rows, :], in_=t1[:]),
                      reads=["E1", "E2"], writes=[(norm_name, i)], dma=True)
            yield

        CUTA = int(os.environ.get("K_CUTA", "99"))
        CUTB = int(os.environ.get("K_CUTB", "99"))

        def limited(g, n, i=None):
            k = 0
            for _ in g:
                k += 1
                if k >= n:
                    break
                yield
            if i is not None and n < 99:
                x = xs[i % 2]
                rows = slice(i * 128, (i + 1) * 128)
                for nm_, o_ in ((raw_name, raw_out), (norm_name, norm_out)):
                    if o_ is not None:
                        S.add("sp", lambda e, o_=o_: e.dma_start(out=o_[rows, :], in_=x[:]),
                              reads=[("xs", i % 2)], writes=[(nm_, i)], dma=True)
            g.close()

        def drain(*gens):
            gens = [g for g in gens if g is not None]
            while gens:
                for g in list(gens):
                    try:
                        next(g)
                    except StopIteration:
                        gens.remove(g)

        if STAGE < 1:
            for i in range(NT):
                do_tile_pt(i)
        elif os.environ.get("K_PIPE", "1") == "1":
            class G:
                def __init__(self, g):
                    self.g, self.n, self.done = g, 0, False

                def step(self):
                    if self.done:
                        return False
                    try:
                        next(self.g)
                        self.n += 1
                        return True
                    except StopIteration:
                        self.done = True
                        return False

            HOLD = int(os.environ.get("K_HOLD", "4"))
            ASTEPS = 12

            def rounds(prims, lead, sec, sec_limit):
                while any(not p.done for p in prims) or (lead is not None and not lead.done):
                    if lead is not None and not lead.done:
                        lead.step()
                    for p in prims:
                        p.step()
                    if sec is not None and (lead is None or lead.done) and sec.n < sec_limit:
                        sec.step()

            a0 = G(stageA(0))
            while a0.step():
                pass
            lead = None
            for i in range(NT):
                sec = G(stageA(i + 1)) if i + 1 < NT else None
                rounds([G(stageATT(i)), G(stageHG(i))], lead, sec, ASTEPS - HOLD)
                rounds([G(stageC(i))], None, sec, ASTEPS - HOLD)
                if sec is not None:
                    while sec.n < ASTEPS - HOLD and sec.step():
                        pass
                lead = sec
            if lead is not None:
                while lead.step():
                    pass
        else:
            for i in range(NT):
                drain(stageA(i))
                drain(stageBC(i))

    for pi_, P in enumerate(passes):
        do_pass(pi_, P)

    fin_reads = []
    for P in passes:
        for k in ("raw_out", "norm_out"):
            nm = P.get(k)
            if nm and nm != "x1":
                fin_reads += [(nm, i) for i in range(NT)]
    fin_reads += [("dbg", nm) for nm in dbg_d]
    S.add("sp", lambda e: e.nop(), reads=fin_reads, writes=["__end"])

    S.ops = [o for o in S.ops if o is not None]
    S.finalize()
    esem = {}
    for en in ("pe", "act", "dve", "pool", "sp"):
        esem[en] = es.enter_context(nc.semaphore("sem_" + en))
    dsem = [es.enter_context(nc.semaphore("dsem%d" % k)) for k in range(S.nd)]
    with nc.Block() as block:
        @block.tensor
        def _(pe):
            S.emit("pe", pe, esem, dsem)

        @block.scalar
        def _(act):
            S.emit("act", act, esem, dsem)

        @block.vector
        def _(dve):
            S.emit("dve", dve, esem, dsem)

        @block.gpsimd
        def _(pool):
            S.emit("pool", pool, esem, dsem)

        @block.sync
        def _(sp):
            S.emit("sp", sp, esem, dsem)
    es.close()
    return nc


def _pk(w, kc):
    n = w.shape[1]
    return np.ascontiguousarray(w.reshape(kc, 128, n).transpose(1, 0, 2).reshape(128, kc * n))


def _layer_inputs(L, sfx, norm_w, w_in, attn_sinks, hgrn_norm_w, w_up_attn, w_up_hgrn, w_out):
    small = np.zeros((128, 19), np.float32)
    small[:, 0:8] = norm_w[L].reshape(8, 128).T
    small[:, 8] = hgrn_norm_w[L]
    small[:, 9:17] = attn_sinks[L][None, :]
    small[:, 17 + L] = 1.0 if L > 0 else 0.0
    return {
        "w_in" + sfx: _pk(w_in[L], 8),
        "w_ua" + sfx: _pk(w_up_attn[L], 4),
        "w_uh" + sfx: _pk(w_up_hgrn[L], 4),
        "w_out" + sfx: _pk(w_out[L], 8),
        "small" + sfx: small,
    }


_NC_CACHE = {}


def _get_nc(key, NT, passes):
    if key not in _NC_CACHE:
        _NC_CACHE[key] = build_nc(NT, passes)
    return _NC_CACHE[key]


def kernel(x, positions, norm_w, w_in, attn_sinks, hgrn_norm_w, w_up_attn, w_up_hgrn, w_out, lb_logits,
           final_norm_w):
    x = np.asarray(x, np.float32)
    positions = np.asarray(positions, np.int32)
    args = [np.asarray(a, np.float32) for a in (norm_w, w_in, attn_sinks, hgrn_norm_w, w_up_attn, w_up_hgrn, w_out)]
    lb_logits = np.asarray(lb_logits, np.float32)
    final_norm_w = np.asarray(final_norm_w, np.float32)
    NT = SEQ // 128
    consts = _consts()
    lbl = np.ascontiguousarray(np.broadcast_to(lb_logits.reshape(1, 2 * 512), (128, 1024)))
    fnw = np.ascontiguousarray(np.broadcast_to(final_norm_w[None, :], (128, D)))
    n_cores = 8
    common = [dict(consts=consts, lbl=lbl, fnw=fnw,
                   pos=np.ascontiguousarray(positions[c % BATCH].reshape(NT, 128).T)) for c in range(n_cores)]
    passes = [dict(src="xin", wsuf="0", raw_out="x1", norm_out=None),
              dict(src="x1", wsuf="1", raw_out=None, norm_out="yout")]
    nc = _get_nc("fused2", NT, passes)
    lw = {}
    for L in range(DEPTH):
        lw.update(_layer_inputs(L, str(L), *args))
    in_maps = []
    for c in range(n_cores):
        m = dict(common[c])
        m.update(lw)
        m["xin"] = np.ascontiguousarray(x[c % BATCH])
        in_maps.append(m)
    res = run_bass_kernel_spmd(nc, in_maps, core_ids=list(range(n_cores)))
    out = np.stack([np.asarray(res.results[b]["yout"]) for b in range(BATCH)], axis=0)
    return out.astype(np.float32)
```

```python
import os
import numpy as np
from contextlib import ExitStack
import concourse.bass as bass
import concourse.mybir as mybir
from concourse.bass_utils import run_bass_kernel_spmd

F32 = mybir.dt.float32
BF16 = mybir.dt.bfloat16
I32 = mybir.dt.int32
AF = mybir.ActivationFunctionType
ALU = mybir.AluOpType

D = 1024
SEQ = 8192
BATCH = 4
DEPTH = 2
INW = 5376
EPS = 1e-6
PI = float(np.pi)
NCONST = 128 * 6 + 8 + 4

SAME_ENGINE_SYNC = os.environ.get("K_SES", "1") == "1"

STAGE = float(os.environ.get('K_STAGE', '9'))


class _Op:
    __slots__ = ("eng", "fn", "dma", "deps", "idx", "sig", "signo", "dslot", "dval", "waits")


class Sched:
    ENGS = ("pe", "act", "dve", "pool", "sp")

    def __init__(self, nd=24, same_sync=True):
        self.ops = []
        self.lw = {}
        self.rd = {}
        self.nd = nd
        self.ndma = 0
        self.same_sync = same_sync

    def add(self, eng, fn, reads=(), writes=(), dma=False):
        o = _Op()
        o.eng, o.fn, o.dma, o.sig, o.signo = eng, fn, dma, False, None
        o.idx = len(self.ops)
        deps = {}
        for k in reads:
            p = self.lw.get(k)
            if p is not None:
                deps[p.idx] = p
        for k in writes:
            p = self.lw.get(k)
            if p is not None:
                deps[p.idx] = p
            r = self.rd.get(k)
            if r:
                for p in r.values():
                    deps[p.idx] = p
        o.deps = list(deps.values())
        for k in reads:
            r = self.rd.setdefault(k, {})
            r[("dma", o.idx) if dma else eng] = o
        for k in writes:
            self.lw[k] = o
            self.rd[k] = {}
        if dma:
            o.dslot = self.ndma % self.nd
            o.dval = 16 * (self.ndma // self.nd + 1)
            self.ndma += 1
        self.ops.append(o)
        return o

    def finalize(self):
        for o in self.ops:
            o.waits = []
            for p in o.deps:
                if p.dma:
                    o.waits.append(p)
                elif p.eng == o.eng and (o.eng == "pe" or not self.same_sync):
                    continue
                else:
                    p.sig = True
                    o.waits.append(p)
        cnt = {e: 0 for e in self.ENGS}
        for o in self.ops:
            if (not o.dma) and o.sig:
                cnt[o.eng] += 1
                o.signo = cnt[o.eng]

    def emit(self, eng, engobj, esem, dsem):
        waited = {}
        for o in self.ops:
            if o.eng != eng:
                continue
            for p in o.waits:
                if p.dma:
                    key, val, sem = ("d", p.dslot), p.dval, dsem[p.dslot]
                else:
                    key, val, sem = ("e", p.eng), p.signo, esem[p.eng]
                if waited.get(key, 0) >= val:
                    continue
                waited[key] = val
                engobj.wait_ge(sem, val)
            if o.dma:
                key = ("d", o.dslot)
                if o.dval > 16 and waited.get(key, 0) < o.dval - 16:
                    engobj.wait_ge(dsem[o.dslot], o.dval - 16)
                    waited[key] = o.dval - 16
                ins = o.fn(engobj)
                ins.then_inc(dsem[o.dslot], 16)
            else:
                ins = o.fn(engobj)
                if o.sig:
                    ins.then_inc(esem[o.eng], 1)


def _consts():
    c = np.zeros((128, NCONST), np.float32)
    r = np.arange(128)
    c[:, 0:128] = np.eye(128, dtype=np.float32)
    kk, qq = r[:, None], r[None, :]
    c[:, 128:256] = (kk > qq)
    c[:, 256:384] = (kk <= qq)
    hm = np.zeros((128, 128), np.float32)
    hm[:64, :64] = (kk[:64] <= qq[:, :64])
    hm[:64, 64:] = 1.0
    hm[64:, 64:] = (kk[64:] <= qq[:, 64:])
    c[:, 384:512] = hm
    same = (kk // 64) == (qq // 64)
    a1 = same * ((kk <= qq).astype(np.float32) - ((kk % 64) <= 31).astype(np.float32))
    c[:, 512:640] = a1
    c[:, 640:768] = (kk > qq)
    inv_freq = np.power(np.float32(500000.0),
                        -np.arange(8, dtype=np.float32) * np.float32(2.0 / 16)).astype(np.float32)
    c[:, 768:776] = inv_freq[None, :]
    c[:, 776] = (r <= 31)
    c[:, 777] = (r <= 95)
    c[:, 778] = (r >= 32) & (r < 96)
    c[:, 779] = 1.0
    return c


def build_nc(NT, passes, dbg_names=()):
    NTOK = NT * 128
    nc = bass.Bass("TRN2", target_bir_lowering=False)
    S = Sched(same_sync=SAME_ENGINE_SYNC)
    es = ExitStack()

    def dram(name, shape, dt, kind):
        return nc.dram_tensor(name, list(shape), dt, kind=kind).ap()

    xin = dram("xin", [NTOK, D], F32, "ExternalInput")
    pos_d = dram("pos", [128, NT], I32, "ExternalInput")
    consts_d = dram("consts", [128, NCONST], F32, "ExternalInput")
    lbl_d = dram("lbl", [128, 2 * 512], F32, "ExternalInput")
    fnw_d = dram("fnw", [128, D], F32, "ExternalInput")
    wd = {}
    for p in passes:
        sfx = p["wsuf"]
        if sfx in wd:
            continue
        wd[sfx] = dict(
            w_in=dram("w_in" + sfx, [128, 8 * INW], F32, "ExternalInput"),
            w_ua=dram("w_ua" + sfx, [128, 4 * D], F32, "ExternalInput"),
            w_uh=dram("w_uh" + sfx, [128, 4 * D], F32, "ExternalInput"),
            w_out=dram("w_out" + sfx, [128, 8 * D], F32, "ExternalInput"),
            small=dram("small" + sfx, [128, 8 + 1 + 8 + 2], F32, "ExternalInput"),
        )
    outs = {}
    for p in passes:
        for k in ("raw_out", "norm_out"):
            nm = p.get(k)
            if nm and nm not in outs:
                if nm == "x1":
                    outs[nm] = dram(nm, [NTOK, D], F32, "Internal")
                else:
                    outs[nm] = dram(nm, [NTOK, D], F32, "ExternalOutput")
    dbg_d = {}
    for nm, shape in dbg_names:
        dbg_d[nm] = dram("dbg_" + nm, shape, F32, "ExternalOutput")

    def sb(name, shape, dt):
        return es.enter_context(nc.sbuf_tensor("s_" + name, list(shape), dt))

    def ps(name, shape, dt):
        return es.enter_context(nc.psum_tensor("p_" + name, list(shape), dt))

    W_in = sb("W_in", [128, 8, INW], BF16)
    W_ua = sb("W_ua", [128, 4, D], BF16)
    W_uh = sb("W_uh", [128, 4, D], BF16)
    W_out = sb("W_out", [128, 8, D], BF16)
    cst = sb("cst", [128, NCONST], F32)
    ident = sb("ident", [128, 128], BF16)
    MK = sb("MK", [128, 2, 128], BF16)
    small = sb("small", [128, 19], F32)
    fnw = sb("fnw", [128, D], F32)
    C0 = sb("C0", [128, 512], F32)
    C1 = sb("C1", [128, 512], F32)
    esink = sb("esink", [128, 8], F32)
    mhalf = sb("mhalf", [128, 8], F32)
    posi = sb("posi", [128, NT], I32)
    posf = sb("posf", [128, NT], F32)
    COS = sb("COS", [128, NT, 8], F32)
    SIN = sb("SIN", [128, NT, 8], F32)

    xs = [sb("xs%d" % i, [128, D], F32) for i in range(2)]
    ss = sb("ss", [128, 4], F32)
    rstd = sb("rstd", [128, 4], F32)
    hb = sb("hb", [128, D], BF16)
    hT = sb("hT", [128, 8, 128], BF16)
    qkf = sb("qkf", [128, 10, 64], F32)
    rt = [sb("rt%d" % i, [128, 10, 8], F32) for i in range(4)]
    qkb = sb("qkb", [128, 12, 64], BF16)
    qT = sb("qT", [128, 4, 128], BF16)
    kT = [sb("kT%d" % i, [128, 2, 128], BF16) for i in range(2)]
    vext = [sb("vext%d" % i, [128, 2, 65], BF16) for i in range(3)]
    zs = [sb("zs%d" % i, [128, 512], BF16) for i in range(2)]
    qhs = [sb("qhs%d" % i, [128, 512], BF16) for i in range(2)]
    tf = [sb("tf%d" % i, [128, 512], F32) for i in range(2)]
    vh = [sb("vh%d" % i, [128, 512], BF16) for i in range(2)]
    gs = [sb("gs%d" % i, [128, 512], BF16) for i in range(2)]
    ta = sb("ta", [128, 2, D], BF16)
    th = sb("th", [128, 2, D], BF16)
    ta32 = ta[:].rearrange("p a b -> p (a b)").bitcast(F32)
    th32 = th[:].rearrange("p a b -> p (a b)").bitcast(F32)
    pT = [[sb("pT%d%d" % (g, r), [128, 2, 2, 128], BF16) for r in range(2)] for g in range(2)]
    den = sb("den", [128, 8], F32)
    rden = sb("rden", [128, 8], F32)
    ab = sb("ab", [128, 512], BF16)
    aT = sb("aT", [128, 4, 128], BF16)
    logf = sb("logf", [128, 512], F32)
    kin = sb("kin", [128, 512], BF16)
    cexp = sb("cexp", [128, 4, 4], F32)
    qg = sb("qg", [128, 512], BF16)
    kg = sb("kg", [128, 512], BF16)
    kS = sb("kS", [128, 512], BF16)
    qgT = sb("qgT", [128, 4, 128], BF16)
    kgT = sb("kgT", [128, 4, 128], BF16)
    kgT0w = sb("kgT0w", [128, 4, 64], BF16)
    Sst = sb("Sst", [128, 4, 128], F32)
    Sa = sb("Sa", [128, 4, 128], BF16)
    Sb = sb("Sb", [128, 4, 128], BF16)
    scb = sb("scb", [128, 4, 128], BF16)
    ssh = sb("ssh", [128, 4], F32)
    rsh = sb("rsh", [128, 4], F32)
    bf_ = sb("bf_", [128, 512], F32)
    af = bf_
    bb = sb("bb", [128, 512], BF16)
    bT = sb("bT", [128, 4, 128], BF16)
    t1 = sb("t1", [128, D], F32)
    t2 = sb("t2", [128, D], F32)
    mb = hb
    mT = sb("mT", [128, 8, 128], BF16)
    assert NT * 8 <= 512
    lbl = ta32.rearrange("p (a b) -> p a b", a=2)
    WS = [t1, t2]
    E1v, E2v, E4v, uv = t1[:, 0:512], t1[:, 512:1024], t2[:, 0:512], t2[:, 512:1024]
    tA = xs[0][:, 0:NT * 8].rearrange("p (t j) -> p t j", j=8)
    tB = xs[1][:, 0:NT * 8].rearrange("p (t j) -> p t j", j=8)
    tC = th32[:, 0:NT * 8].rearrange("p (t j) -> p t j", j=8)
    tI = t1[:, 0:NT * 8].bitcast(I32).rearrange("p (t j) -> p t j", j=8)

    Fb = [ps("F%d" % i, [128, 512], F32) for i in range(6)]
    Tb = [ps("T%d" % i, [128, 8, 128], BF16) for i in range(2)]

    c_ident = cst[:, 0:128]
    c_maskP = cst[:, 128:256]
    c_maskC = cst[:, 256:384]
    c_hm = cst[:, 384:512]
    c_a1 = cst[:, 512:640]
    c_a4 = cst[:, 640:768]
    c_invf = cst[:, 768:776]
    c_ind = cst[:, 776:780]

    tr_ctr = [0]

    def next_T():
        t = tr_ctr[0] % int(os.environ.get("K_NT", "2"))
        tr_ctr[0] += 1
        return t

    def dbg(name, key, ap_fn):
        if name in dbg_d:
            S.add("sp", lambda e: e.dma_start(out=dbg_d[name], in_=ap_fn()), reads=[key],
                  writes=[("dbg", name)], dma=True)

    S.add("sp", lambda e: e.dma_start(out=cst[:], in_=consts_d), writes=["cst"], dma=True)
    S.add("sp", lambda e: e.dma_start(out=posi[:], in_=pos_d), writes=["posi"], dma=True)
    S.add("sp", lambda e: e.dma_start(out=fnw[:], in_=fnw_d), writes=["fnw"], dma=True)
    S.add("dve", lambda e: e.tensor_copy(out=ident[:], in_=c_ident), reads=["cst"], writes=["ident"])
    S.add("dve", lambda e: e.tensor_copy(out=MK[:, 0, :], in_=c_maskP), reads=["cst"], writes=["MK0"])
    S.add("dve", lambda e: e.tensor_copy(out=MK[:, 1, :], in_=c_maskC), reads=["cst", "MK0"], writes=["MK"])
    S.add("dve", lambda e: e.memset(mhalf[:], -0.5), writes=["mhalf"])
    S.add("dve", lambda e: e.memset(scb[:], 0.0), writes=["scb"])
    for sl in range(3):
        S.add("dve", lambda e, sl=sl: e.memset(vext[sl][:], 1.0), writes=[("vext", sl)])

    S.add("dve", lambda e: e.tensor_copy(out=posf[:], in_=posi[:]), reads=["posi"], writes=["posf"])
    S.add("dve", lambda e: e.tensor_tensor(
        out=tA, in0=posf[:].unsqueeze(2).to_broadcast([128, NT, 8]),
        in1=c_invf.unsqueeze(1).to_broadcast([128, NT, 8]), op=ALU.mult),
        reads=["posf", "cst"], writes=[("xs", 0)])

    def range_reduce(src_key, shift, dst, dst_key):
        S.add("dve", lambda e: e.tensor_scalar(out=tB, in0=tA, scalar1=float(shift), scalar2=None, op0=ALU.add),
              reads=[src_key], writes=[("xs", 1)])
        S.add("dve", lambda e: e.tensor_scalar(out=tI, in0=tB, scalar1=float(1.0 / (2 * PI)), scalar2=None,
                                                op0=ALU.mult), reads=[("xs", 1)], writes=["E1"])
        S.add("dve", lambda e: e.tensor_copy(out=tC, in_=tI), reads=["E1"], writes=["thall"])
        S.add("dve", lambda e: e.scalar_tensor_tensor(out=tB, in0=tC, scalar=float(-2 * PI), in1=tB,
                                                       op0=ALU.mult, op1=ALU.add), reads=["thall", ("xs", 1)], writes=[("xs", 1)])
        S.add("dve", lambda e: e.tensor_single_scalar(out=tC, in_=tB, scalar=PI, op=ALU.is_gt),
              reads=[("xs", 1)], writes=["thall"])
        S.add("dve", lambda e: e.scalar_tensor_tensor(out=tB, in0=tC, scalar=float(-2 * PI), in1=tB,
                                                       op0=ALU.mult, op1=ALU.add), reads=["thall", ("xs", 1)], writes=[("xs", 1)])
        S.add("dve", lambda e: e.tensor_single_scalar(out=tC, in_=tB, scalar=-PI, op=ALU.is_lt),
              reads=[("xs", 1)], writes=["thall"])
        S.add("dve", lambda e: e.scalar_tensor_tensor(out=tB, in0=tC, scalar=float(2 * PI), in1=tB,
                                                       op0=ALU.mult, op1=ALU.add), reads=["thall", ("xs", 1)], writes=[("xs", 1)])
        S.add("dve", lambda e: e.tensor_scalar(out=tB, in0=tB, scalar1=3.1415925, scalar2=-3.1415925,
                                                op0=ALU.min, op1=ALU.max), reads=[("xs", 1)], writes=[("xs", 1)])
        S.add("act", lambda e: e.activation(out=dst[:], in_=tB, func=AF.Sin), reads=[("xs", 1)], writes=[dst_key])

    range_reduce(("xs", 0), 0.0, SIN, "SIN")
    range_reduce(("xs", 0), PI / 2, COS, "COS")

    def do_pass(pi_, P):
        wsrc = wd[P["wsuf"]]
        src = xin if P["src"] == "xin" else outs[P["src"]]
        src_name = P["src"]
        raw_out = outs.get(P.get("raw_out")) if P.get("raw_out") else None
        raw_name = P.get("raw_out")
        norm_out = outs.get(P.get("norm_out")) if P.get("norm_out") else None
        norm_name = P.get("norm_out")

        S.add("sp", lambda e: e.dma_start(out=small[:], in_=wsrc["small"]), writes=["small"], dma=True)
        nwv = small[:, 0:8]
        hnw = small[:, 8:9]
        sinks = small[:, 9:17]
        lsel = small[:, 17:19]
        S.add("act", lambda e: e.activation(out=esink[:], in_=sinks, func=AF.Exp), reads=["small"], writes=["esink"])
        LK = [("ta", pp, off) for pp in range(2) for off in (0, 512)]
        S.add("sp", lambda e: e.dma_start(out=ta32, in_=lbl_d), writes=LK, dma=True)
        S.add("act", lambda e: e.activation(out=ta32, in_=ta32, func=AF.Exp), reads=LK, writes=LK)
        S.add("dve", lambda e: e.tensor_tensor(out=C0[:], in0=lbl[:, 0, :], in1=lbl[:, 1, :], op=ALU.add),
              reads=LK, writes=["C0"])
        S.add("dve", lambda e: e.reciprocal(out=C0[:], in_=C0[:]), reads=["C0"], writes=["C0"])
        S.add("dve", lambda e: e.tensor_scalar(out=C1[:], in0=lbl[:, 0, :], scalar1=lsel[:, 0:1], scalar2=None,
                                                op0=ALU.mult), reads=LK + ["small"], writes=["C1"])
        S.add("dve", lambda e: e.scalar_tensor_tensor(out=C1[:], in0=lbl[:, 1, :], scalar=lsel[:, 1:2], in1=C1[:],
                                                       op0=ALU.mult, op1=ALU.add), reads=LK + ["small", "C1"],
              writes=["C1"])
        S.add("dve", lambda e: e.tensor_tensor(out=C1[:], in0=C1[:], in1=C0[:], op=ALU.mult),
              reads=["C1", "C0"], writes=["C1"])
        S.add("dve", lambda e: e.tensor_scalar(out=C0[:], in0=C1[:], scalar1=0.5, scalar2=0.5, op0=ALU.mult,
                                                op1=ALU.add), reads=["C1"], writes=["C0"])
        S.add("dve", lambda e: e.tensor_scalar(out=C1[:], in0=C1[:], scalar1=-0.5, scalar2=0.5, op0=ALU.mult,
                                                op1=ALU.add), reads=["C1"], writes=["C1"])

        S.add("dve", lambda e: e.memset(ss[:, 2:3], 0.0), writes=["Wdone"])
        pieces = []
        for k in range(8):
            for q4 in range(6):
                wdt_ = 1024 if q4 < 5 else 256
                pieces.append((wsrc["w_in"][:, k * INW + q4 * 1024: k * INW + q4 * 1024 + wdt_],
                               W_in[:, k, q4 * 1024:q4 * 1024 + wdt_], ("nw", k), wdt_))
        for k in range(4):
            pieces.append((wsrc["w_ua"][:, k * D:(k + 1) * D], W_ua[:, k, :], ("half",), 1024))
        for k in range(4):
            pieces.append((wsrc["w_uh"][:, k * D:(k + 1) * D], W_uh[:, k, :], ("hnw",), 1024))
        for k in range(8):
            pieces.append((wsrc["w_out"][:, k * D:(k + 1) * D], W_out[:, k, :], ("half",), 1024))
        if STAGE < 1:
            pieces = []
        for n, (srcap, dstap, scl, wdt) in enumerate(pieces):
            st = n % 2
            S.add("sp", lambda e, srcap=srcap, st=st, wdt=wdt: e.dma_start(out=WS[st][:, 0:wdt], in_=srcap),
                  writes=[("E1", "E2"), ("E4", "u")][st], dma=True)
            if len(dstap.shape) == 3:
                dstv = dstap.rearrange("p a b -> p (a b)")
            else:
                dstv = dstap
            if scl is None:
                sc = 1.0
            elif scl[0] == "nw":
                sc = nwv[:, scl[1]:scl[1] + 1]
            elif scl[0] == "hnw":
                sc = hnw
            else:
                sc = 0.5
            wkey = ("W", n)
            engs = tuple(os.environ.get("K_WENG", "dve,act,pool").split(","))
            eg = engs[n % len(engs)]
            if eg == "act":
                S.add("act", lambda e, dstv=dstv, st=st, wdt=wdt, sc=sc: e.activation(
                    out=dstv, in_=WS[st][:, 0:wdt], func=AF.Copy, scale=sc),
                    reads=[("E1", "E2"), ("E4", "u")][st] + ("small", "Wdone"), writes=[wkey])
            else:
                S.add(eg, lambda e, dstv=dstv, st=st, wdt=wdt, sc=sc: e.tensor_scalar(
                    out=dstv, in0=WS[st][:, 0:wdt], scalar1=sc, scalar2=None, op0=ALU.mult),
                    reads=[("E1", "E2"), ("E4", "u")][st] + ("small", "Wdone"), writes=[wkey])
        S.add("dve", lambda e: e.memset(ss[:, 3:4], 0.0), reads=[("W", n) for n in range(len(pieces))],
              writes=["Wdone"])
        S.add("dve", lambda e: e.memset(Sst[:], 0.0), writes=["Sst"])

        def do_tile_pt(i):
            x = xs[i % 2]
            rows = slice(i * 128, (i + 1) * 128)
            S.add("sp", lambda e: e.dma_start(out=x[:], in_=src[rows, :]), writes=[("xs", i % 2)], dma=True)
            for nm_, o_ in ((raw_name, raw_out), (norm_name, norm_out)):
                if o_ is not None:
                    S.add("sp", lambda e, o_=o_: e.dma_start(out=o_[rows, :], in_=x[:]),
                          reads=[("xs", i % 2)], writes=[(nm_, i)], dma=True)

        chunks = [(0, 512), (512, 256), (768, 512), (1280, 512), (1792, 512), (2304, 512), (2816, 512),
                  (3328, 512), (3840, 512), (4352, 512), (4864, 512)]

        def stageA(i):
            par = i % 2
            x = xs[par]
            xkey = ("xs", par)
            v3 = i % 3
            rows = slice(i * 128, (i + 1) * 128)
            rd = [(src_name, i)] if src_name != "xin" else []
            S.add("sp", lambda e: e.dma_start(out=x[:], in_=src[rows, :]), reads=rd, writes=[xkey], dma=True)
            S.add("act", lambda e: e.activation(out=hb[:], in_=x[:], func=AF.Square, accum_out=ss[:, 0:1]),
                  reads=[xkey], writes=["hb", "ss0"])
            S.add("pool", lambda e: e.tensor_scalar(out=rstd[:, 0:1], in0=ss[:, 0:1], scalar1=1.0 / D, scalar2=EPS,
                                                     op0=ALU.mult, op1=ALU.add), reads=["ss0"], writes=["rstd0a"])
            S.add("pool", lambda e: e.tensor_tensor(out=rstd[:, 0:1], in0=rstd[:, 0:1], in1=mhalf[:, 0:1], op=ALU.pow),
                  reads=["rstd0a", "mhalf"], writes=["rstd0"])
            S.add("act", lambda e: e.activation(out=hb[:], in_=x[:], func=AF.Copy, scale=rstd[:, 0:1]),
                  reads=[xkey, "rstd0"], writes=["hb"])
            for k in range(8):
                S.add("pe", lambda e, k=k: e.transpose(out=Tb[0][:, k, :], in_=hb[:, k * 128:(k + 1) * 128],
                                                       identity=ident[:]),
                      reads=["hb", "ident"], writes=[("T", 0)])
            S.add("dve", lambda e: e.tensor_copy(out=hT[:], in_=Tb[0][:]), reads=[("T", 0)], writes=["hT"])
            yield
            for c, (c0, cw) in enumerate(chunks):
                bk = c % 2
                for k in range(8):
                    S.add("pe", lambda e, k=k, c0=c0, cw=cw, bk=bk: e.matmul(
                        Fb[bk][:, 0:cw], lhsT=hT[:, k, :], rhs=W_in[:, k, c0:c0 + cw], start=(k == 0), stop=(k == 7)),
                        reads=["hT", "Wdone"], writes=[("F", bk)])
                Fk = ("F", bk)
                if c == 0:
                    S.add("dve", lambda e, bk=bk: e.tensor_copy(out=qkf[:, 0:8, :].rearrange("p a b -> p (a b)"),
                                                                 in_=Fb[bk][:, 0:512]), reads=[Fk], writes=["qkf_q"])
                elif c == 1:
                    S.add("dve", lambda e, bk=bk: e.tensor_copy(out=qkf[:, 8:10, :].rearrange("p a b -> p (a b)"),
                                                                 in_=Fb[bk][:, 0:128]), reads=[Fk], writes=["qkf_k"])
                    S.add("dve", lambda e, bk=bk: e.tensor_copy(
                        out=vext[v3][:, :, 0:64], in_=Fb[bk][:, 128:256].rearrange("p (g d) -> p g d", g=2)),
                        reads=[Fk], writes=[("vext", v3)])
                elif c in (2, 3, 6):
                    dst, dk = {2: (zs, "zs"), 3: (qhs, "qhs"), 6: (gs, "gs")}[c]
                    S.add("act", lambda e, bk=bk, dst=dst: e.activation(out=dst[par][:], in_=Fb[bk][:, 0:512],
                                                                        func=AF.Tanh, scale=0.5),
                          reads=[Fk], writes=[(dk, par)])
                    S.add("dve", lambda e, bk=bk, dst=dst: e.scalar_tensor_tensor(
                        out=dst[par][:], in0=dst[par][:], scalar=1.0, in1=Fb[bk][:, 0:512], op0=ALU.add, op1=ALU.mult),
                        reads=[Fk, (dk, par)], writes=[(dk, par)])
                elif c == 4:
                    S.add("act", lambda e, bk=bk: e.activation(out=tf[par][:], in_=Fb[bk][:, 0:512], func=AF.Tanh,
                                                               scale=0.5), reads=[Fk], writes=[("tf", par)])
                    S.add("dve", lambda e: e.tensor_tensor(out=tf[par][:], in0=tf[par][:], in1=C1[:], op=ALU.mult),
                          reads=[("tf", par), "C1"], writes=[("tf", par)])
                    S.add("dve", lambda e: e.tensor_tensor(out=tf[par][:], in0=tf[par][:], in1=C0[:], op=ALU.add),
                          reads=[("tf", par), "C0"], writes=[("tf", par)])
                    S.add("act", lambda e: e.activation(out=logf[:], in_=tf[par][:], func=AF.Ln),
                          reads=[("tf", par)], writes=["logf"])
                    S.add("dve", lambda e: e.tensor_scalar(out=kin[:], in0=tf[par][:], scalar1=-1.0, scalar2=1.0,
                                                            op0=ALU.mult, op1=ALU.add),
                          reads=[("tf", par)], writes=["kin"])
                elif c == 5:
                    S.add("act", lambda e, bk=bk: e.activation(out=vh[par][:], in_=Fb[bk][:, 0:512], func=AF.Copy),
                          reads=[Fk], writes=[("vh", par)])
                else:
                    dst, dk = (ta, "ta") if c in (7, 8) else (th, "th")
                    off = 0 if c in (7, 9) else 512
                    wk = [(dk, par, off)] + (["thall"] if dk == "th" else [])
                    S.add("act", lambda e, bk=bk, dst=dst, off=off: e.activation(
                        out=dst[:, par, off:off + 512], in_=Fb[bk][:, 0:512], func=AF.Tanh, scale=0.5),
                        reads=[Fk], writes=wk)
                yield

        def stageATT(i):
            first = (i == 0)
            par = i % 2
            x = xs[par]
            xkey = ("xs", par)
            cur, prv = i % 2, (i - 1) % 2
            v3, v3p = i % 3, (i - 1) % 3
            rows = slice(i * 128, (i + 1) * 128)
            T = 1
            cosb = COS[:, i, :].unsqueeze(1).to_broadcast([128, 10, 8])
            sinb = SIN[:, i, :].unsqueeze(1).to_broadcast([128, 10, 8])
            x1v, x2v = qkf[:, :, 0:8], qkf[:, :, 8:16]
            qk_keys = ["qkf_q", "qkf_k"]
            S.add("pool", lambda e: e.tensor_tensor(out=rt[0][:], in0=x1v, in1=cosb, op=ALU.mult),
                  reads=qk_keys + ["COS"], writes=["rt0"])
            S.add("pool", lambda e: e.tensor_tensor(out=rt[1][:], in0=x2v, in1=sinb, op=ALU.mult),
                  reads=qk_keys + ["SIN"], writes=["rt1"])
            S.add("pool", lambda e: e.tensor_tensor(out=rt[2][:], in0=x2v, in1=cosb, op=ALU.mult),
                  reads=qk_keys + ["COS"], writes=["rt2"])
            S.add("pool", lambda e: e.tensor_tensor(out=rt[3][:], in0=x1v, in1=sinb, op=ALU.mult),
                  reads=qk_keys + ["SIN"], writes=["rt3"])
            S.add("pool", lambda e: e.tensor_tensor(out=x1v, in0=rt[0][:], in1=rt[1][:], op=ALU.subtract),
                  reads=["rt0", "rt1", "rt2", "rt3"], writes=["qkf_r1"])
            S.add("pool", lambda e: e.tensor_tensor(out=x2v, in0=rt[2][:], in1=rt[3][:], op=ALU.add),
                  reads=["rt2", "rt3", "qkf_r1"], writes=["qkf_r2"])
            S.add("dve", lambda e: e.tensor_scalar(out=qkb[:, 0:8, :], in0=qkf[:, 0:8, :], scalar1=0.125, scalar2=None,
                                                    op0=ALU.mult), reads=["qkf_q", "qkf_r1", "qkf_r2"], writes=["qkb_q"])
            S.add("dve", lambda e: e.tensor_copy(
                out=qkb[:, 8:12, :].rearrange("p (g r) d -> p g r d", g=2),
                in_=qkf[:, 8:10, :].unsqueeze(2).to_broadcast([128, 2, 2, 64])),
                reads=["qkf_k", "qkf_r1", "qkf_r2"], writes=["qkb_k"])
            for c in range(6):
                S.add("pe", lambda e, c=c: e.transpose(
                    out=Tb[T][:, c, :], in_=qkb[:, 2 * c:2 * c + 2, :].rearrange("p a b -> p (a b)"),
                    identity=ident[:]), reads=["qkb_q", "qkb_k", "ident"], writes=[("T", T)])
            S.add("dve", lambda e: e.tensor_copy(out=qT[:], in_=Tb[T][:, 0:4, :]), reads=[("T", T)], writes=["qT"])
            S.add("dve", lambda e: e.tensor_copy(out=kT[cur][:], in_=Tb[T][:, 4:6, :]),
                  reads=[("T", T)], writes=[("kT", cur)])
            yield
            js = [1] if first else [0, 1]
            for g in range(2):
                for r in range(2):
                    bk = 2 + r
                    for j in js:
                        ksl = prv if j == 0 else cur
                        S.add("pe", lambda e, g=g, r=r, j=j, ksl=ksl, bk=bk: e.matmul(
                            Fb[bk][:, j * 256:(j + 1) * 256],
                            lhsT=kT[ksl][64 * r:64 * r + 64, g, :],
                            rhs=qT[64 * r:64 * r + 64, 2 * g:2 * g + 2, :].rearrange("p a b -> p (a b)"),
                            start=True, stop=True),
                            reads=[("kT", ksl), "qT"], writes=[("F", bk)])
                    lo = js[0] * 256
                    S.add("act", lambda e, g=g, r=r, bk=bk, lo=lo: e.activation(
                        out=pT[g][r][:].rearrange("p a b c -> p (a b c)")[:, lo:512], in_=Fb[bk][:, lo:512],
                        func=AF.Exp), reads=[("F", bk)], writes=[("pT", g, r)])
                    j0 = js[0]
                    S.add("pool", lambda e, g=g, r=r, j0=j0: e.tensor_tensor(
                        out=pT[g][r][:, j0:2, :, :], in0=pT[g][r][:, j0:2, :, :],
                        in1=MK[:, j0:2, :].unsqueeze(2).to_broadcast([128, 2 - j0, 2, 128]), op=ALU.mult),
                        reads=[("pT", g, r), "MK"], writes=[("pT", g, r)])
                yield
            obanks = {0: 2, 1: 3}
            for g in range(2):
                bk = obanks[g]
                for hl in range(4):
                    r, cc = hl % 2, hl // 2
                    for jn, j in enumerate(js):
                        vsl = v3p if j == 0 else v3
                        S.add("pe", lambda e, g=g, r=r, cc=cc, j=j, hl=hl, vsl=vsl, bk=bk, st_=(jn == 0),
                              sp_=(jn == len(js) - 1): e.matmul(
                            Fb[bk][:, hl * 65:(hl + 1) * 65], lhsT=pT[g][r][:, j, cc, :], rhs=vext[vsl][:, g, :],
                            start=st_, stop=sp_),
                            reads=[("pT", g, r), ("vext", vsl)], writes=[("F", bk)])
                ov = Fb[bk][:, 0:260].rearrange("p (h d) -> p h d", d=65)
                S.add("dve", lambda e, g=g, ov=ov: e.tensor_tensor(
                    out=den[:, 4 * g:4 * g + 4], in0=ov[:, :, 64], in1=esink[:, 4 * g:4 * g + 4], op=ALU.add),
                    reads=[("F", bk), "esink"], writes=[("den", g)])
                S.add("dve", lambda e, g=g: e.reciprocal(out=rden[:, 4 * g:4 * g + 4], in_=den[:, 4 * g:4 * g + 4]),
                      reads=[("den", g)], writes=[("rden", g)])
                S.add("dve", lambda e, g=g, ov=ov: e.tensor_tensor(
                    out=af[:, 256 * g:256 * g + 256].rearrange("p (h d) -> p h d", d=64), in0=ov[:, :, 0:64],
                    in1=rden[:, 4 * g:4 * g + 4].unsqueeze(2).to_broadcast([128, 4, 64]), op=ALU.mult),
                    reads=[("F", bk), ("rden", g)], writes=[("af", g)])
            S.add("pool", lambda e: e.tensor_tensor(out=ab[:], in0=af[:], in1=zs[par][:], op=ALU.mult),
                  reads=[("af", 0), ("af", 1), ("zs", par)], writes=["ab"])
            yield
            for c in range(4):
                S.add("pe", lambda e, c=c: e.transpose(out=Tb[T][:, c, :], in_=ab[:, c * 128:(c + 1) * 128],
                                                       identity=ident[:]),
                      reads=["ab", "ident"], writes=[("T", T)])
            S.add("dve", lambda e: e.tensor_copy(out=aT[:], in_=Tb[T][:, 0:4, :]), reads=[("T", T)], writes=["aT"])
            yield
            for n in range(2):
                bk = (2, 3)[n]
                for k in range(4):
                    S.add("pe", lambda e, n=n, k=k, bk=bk: e.matmul(Fb[bk][:, :], lhsT=aT[:, k, :],
                                                                    rhs=W_ua[:, k, n * 512:(n + 1) * 512],
                                                                    start=(k == 0), stop=(k == 3)),
                          reads=["aT", "Wdone"], writes=[("F", bk)])
                S.add("dve", lambda e, n=n, bk=bk: e.scalar_tensor_tensor(
                    out=t1[:, n * 512:(n + 1) * 512], in0=ta[:, par, n * 512:(n + 1) * 512], scalar=1.0,
                    in1=Fb[bk][:, :], op0=ALU.add, op1=ALU.mult),
                    reads=[("ta", par, n * 512), ("F", bk)], writes=[("E1", "E2")[n]])
            yield

        def stageHG(i):
            first = (i == 0)
            par = i % 2
            x = xs[par]
            xkey = ("xs", par)
            cur, prv = i % 2, (i - 1) % 2
            v3, v3p = i % 3, (i - 1) % 3
            rows = slice(i * 128, (i + 1) * 128)
            T = 0
            S.add("pe", lambda e: e.matmul(Fb[4][:, :], lhsT=c_a1, rhs=logf[:], start=True, stop=True),
                  reads=["logf", "cst"], writes=[("F", 4)])
            S.add("act", lambda e: e.activation(out=E1v, in_=Fb[4][:, :], func=AF.Exp), reads=[("F", 4)], writes=["E1"])
            S.add("act", lambda e: e.activation(out=E2v, in_=Fb[4][:, :], func=AF.Exp, scale=-1.0),
                  reads=[("F", 4)], writes=["E2"])
            S.add("pe", lambda e: e.matmul(Fb[5][:, :], lhsT=c_a4, rhs=logf[:], start=True, stop=True),
                  reads=["logf", "cst"], writes=[("F", 5)])
            S.add("act", lambda e: e.activation(out=E4v, in_=Fb[5][:, :], func=AF.Exp), reads=[("F", 5)], writes=["E4"])
            for h in range(4):
                S.add("pe", lambda e, h=h: e.matmul(Fb[4][:, 4 * h:4 * h + 4], lhsT=logf[:, h * 128:(h + 1) * 128],
                                                    rhs=c_ind, start=True, stop=True),
                      reads=["logf", "cst"], writes=[("F", 4)])
            S.add("act", lambda e: e.activation(out=cexp[:].rearrange("p a b -> p (a b)"), in_=Fb[4][:, 0:16],
                                                func=AF.Exp), reads=[("F", 4)], writes=["cexp"])
            S.add("dve", lambda e: e.tensor_tensor(out=qg[:], in0=qhs[par][:], in1=E1v, op=ALU.mult),
                  reads=[("qhs", par), "E1"], writes=["qg"])
            S.add("pool", lambda e: e.tensor_tensor(out=kg[:], in0=kin[:], in1=E2v, op=ALU.mult),
                  reads=["kin", "E2"], writes=["kg"])
            S.add("pool", lambda e: e.tensor_tensor(out=kS[:], in0=kin[:], in1=E4v, op=ALU.mult),
                  reads=["kin", "E4"], writes=["kS"])
            yield
            for c in range(4):
                S.add("pe", lambda e, c=c: e.transpose(out=Tb[T][:, c, :], in_=qg[:, c * 128:(c + 1) * 128],
                                                       identity=ident[:]),
                      reads=["qg", "ident"], writes=[("T", T)])
            for c in range(4):
                S.add("pe", lambda e, c=c: e.transpose(out=Tb[T][:, 4 + c, :], in_=kg[:, c * 128:(c + 1) * 128],
                                                       identity=ident[:]),
                      reads=["kg", "ident"], writes=[("T", T)])
            S.add("dve", lambda e: e.tensor_copy(out=qgT[:], in_=Tb[T][:, 0:4, :]), reads=[("T", T)], writes=["qgT"])
            S.add("dve", lambda e: e.tensor_copy(out=kgT[:], in_=Tb[T][:, 4:8, :]), reads=[("T", T)], writes=["kgT"])
            S.add("dve", lambda e: e.tensor_tensor(
                out=kgT0w[:], in0=kgT[:, :, 0:64], in1=cexp[:, :, 2:3].to_broadcast([128, 4, 64]), op=ALU.mult),
                reads=["kgT", "cexp"], writes=["kgT0w"])
            if not first:
                S.add("dve", lambda e: e.tensor_tensor(
                    out=Sa[:], in0=Sst[:], in1=cexp[:, :, 0:1].to_broadcast([128, 4, 128]), op=ALU.mult),
                    reads=["Sst", "cexp"], writes=["Sa"])
                S.add("pool", lambda e: e.tensor_tensor(
                    out=Sb[:], in0=Sst[:], in1=cexp[:, :, 1:2].to_broadcast([128, 4, 128]), op=ALU.mult),
                    reads=["Sst", "cexp"], writes=["Sb"])
            yield
            scv = Fb[5][:, :].rearrange("p (h t) -> p h t", h=4)
            for h in range(4):
                S.add("pe", lambda e, h=h: e.matmul(scv[0:64, h, 0:64], lhsT=kgT[:, h, 0:64], rhs=qgT[:, h, 0:64],
                                                    start=True, stop=True),
                      reads=["kgT", "qgT"], writes=[("F", 5)])
                S.add("pe", lambda e, h=h: e.matmul(scv[0:64, h, 64:128], lhsT=kgT0w[:, h, :], rhs=qgT[:, h, 64:128],
                                                    start=True, stop=True),
                      reads=["kgT0w", "qgT"], writes=[("F", 5)])
                S.add("pe", lambda e, h=h: e.matmul(scv[64:128, h, 64:128], lhsT=kgT[:, h, 64:128],
                                                    rhs=qgT[:, h, 64:128], start=True, stop=True),
                      reads=["kgT", "qgT"], writes=[("F", 5)])
            S.add("dve", lambda e: e.tensor_tensor(
                out=scb[0:64, :, :], in0=scv[0:64, :, :],
                in1=c_hm[0:64, :].unsqueeze(1).to_broadcast([64, 4, 128]), op=ALU.mult),
                reads=[("F", 5), "cst", "scb"], writes=["scb_a"])
            S.add("dve", lambda e: e.tensor_tensor(
                out=scb[64:128, :, 64:128], in0=scv[64:128, :, 64:128],
                in1=c_hm[64:128, 64:128].unsqueeze(1).to_broadcast([64, 4, 64]), op=ALU.mult),
                reads=[("F", 5), "cst", "scb"], writes=["scb_b"])
            for h in range(4):
                S.add("pe", lambda e, h=h: e.matmul(Fb[4][:, h * 128:(h + 1) * 128], lhsT=kS[:, h * 128:(h + 1) * 128],
                                                    rhs=vh[par][:, h * 128:(h + 1) * 128], start=True, stop=True),
                      reads=["kS", ("vh", par), "cexp"], writes=[("F", 4)])
            yield
            for h in range(4):
                S.add("pe", lambda e, h=h: e.matmul(Fb[5][:, h * 128:(h + 1) * 128], lhsT=scb[:, h, :],
                                                    rhs=vh[par][:, h * 128:(h + 1) * 128], start=True, stop=first),
                      reads=["scb_a", "scb_b", ("vh", par)], writes=[("F", 5)])
                if not first:
                    S.add("pe", lambda e, h=h: e.matmul(Fb[5][0:64, h * 128:(h + 1) * 128], lhsT=qgT[:, h, 0:64],
                                                        rhs=Sa[:, h, :], start=False, stop=True),
                          reads=["qgT", "Sa"], writes=[("F", 5)])
                    S.add("pe", lambda e, h=h: e.matmul(Fb[5][64:128, h * 128:(h + 1) * 128], lhsT=qgT[:, h, 64:128],
                                                        rhs=Sb[:, h, :], start=False, stop=True),
                          reads=["qgT", "Sb"], writes=[("F", 5)])
            Sv = Sst[:].rearrange("p a b -> p (a b)")
            if first:
                S.add("dve", lambda e: e.tensor_copy(out=Sv, in_=Fb[4][:, :]), reads=[("F", 4), "Sst"], writes=["Sst"])
            else:
                S.add("dve", lambda e: e.tensor_tensor(
                    out=Sst[:], in0=Sst[:], in1=cexp[:, :, 3:4].to_broadcast([128, 4, 128]), op=ALU.mult),
                    reads=["Sst", "cexp", "Sa", "Sb"], writes=["Sst"])
                S.add("dve", lambda e: e.tensor_tensor(out=Sv, in0=Sv, in1=Fb[4][:, :], op=ALU.add),
                      reads=["Sst", ("F", 4)], writes=["Sst"])
            for h in range(4):
                S.add("act", lambda e, h=h: e.activation(out=bb[:, h * 128:(h + 1) * 128],
                                                         in_=Fb[5][:, h * 128:(h + 1) * 128], func=AF.Square,
                                                         accum_out=ssh[:, h:h + 1]),
                      reads=[("F", 5)], writes=["bb", ("ssh", h)])
            S.add("pool", lambda e: e.tensor_scalar(out=rsh[:], in0=ssh[:], scalar1=4.0 / 128, scalar2=16.0 * EPS,
                                                     op0=ALU.mult, op1=ALU.add),
                  reads=[("ssh", h) for h in range(4)], writes=["rsha"])
            S.add("pool", lambda e: e.tensor_tensor(out=rsh[:], in0=rsh[:], in1=mhalf[:, 0:4], op=ALU.pow),
                  reads=["rsha", "mhalf"], writes=["rsh"])
            S.add("dve", lambda e: e.tensor_tensor(
                out=bf_[:].rearrange("p (h d) -> p h d", h=4), in0=Fb[5][:, :].rearrange("p (h d) -> p h d", h=4),
                in1=rsh[:].unsqueeze(2).to_broadcast([128, 4, 128]), op=ALU.mult),
                reads=[("F", 5), "rsh"], writes=[("af", 0), ("af", 1)])
            S.add("pool", lambda e: e.tensor_tensor(out=bb[:], in0=bf_[:], in1=gs[par][:], op=ALU.mult),
                  reads=[("af", 0), ("af", 1), ("gs", par)], writes=["bb"])
            yield
            for c in range(4):
                S.add("pe", lambda e, c=c: e.transpose(out=Tb[T][:, c, :], in_=bb[:, c * 128:(c + 1) * 128],
                                                       identity=ident[:]),
                      reads=["bb", "ident"], writes=[("T", T)])
            S.add("dve", lambda e: e.tensor_copy(out=bT[:], in_=Tb[T][:, 0:4, :]), reads=[("T", T)], writes=["bT"])
            yield
            for n in range(2):
                bk = (4, 5)[n]
                for k in range(4):
                    S.add("pe", lambda e, n=n, k=k, bk=bk: e.matmul(Fb[bk][:, :], lhsT=bT[:, k, :],
                                                                    rhs=W_uh[:, k, n * 512:(n + 1) * 512],
                                                                    start=(k == 0), stop=(k == 3)),
                          reads=["bT", "Wdone"], writes=[("F", bk)])
                S.add("dve", lambda e, n=n, bk=bk: e.scalar_tensor_tensor(
                    out=t2[:, n * 512:(n + 1) * 512], in0=th[:, par, n * 512:(n + 1) * 512], scalar=1.0,
                    in1=Fb[bk][:, :], op0=ALU.add, op1=ALU.mult),
                    reads=[("th", par, n * 512), ("F", bk)], writes=[("E4", "u")[n]])
            yield

        def stageC(i):
            first = (i == 0)
            par = i % 2
            x = xs[par]
            xkey = ("xs", par)
            cur, prv = i % 2, (i - 1) % 2
            v3, v3p = i % 3, (i - 1) % 3
            rows = slice(i * 128, (i + 1) * 128)
            T = 1
            for n in range(2):
                S.add("pool", lambda e, n=n: e.tensor_tensor(
                    out=mb[:, n * 512:(n + 1) * 512], in0=t1[:, n * 512:(n + 1) * 512],
                    in1=t2[:, n * 512:(n + 1) * 512], op=ALU.add),
                    reads=[("E1", "E2")[n], ("E4", "u")[n]], writes=["hb"])
            yield
            for k in range(8):
                S.add("pe", lambda e, k=k: e.transpose(out=Tb[T][:, k, :], in_=mb[:, k * 128:(k + 1) * 128],
                                                       identity=ident[:]),
                      reads=["hb", "ident"], writes=[("T", T)])
            S.add("dve", lambda e: e.tensor_copy(out=mT[:], in_=Tb[T][:]), reads=[("T", T)], writes=["mT"])
            yield
            for n in range(2):
                bk = (2, 3)[n]
                for k in range(8):
                    S.add("pe", lambda e, n=n, k=k, bk=bk: e.matmul(Fb[bk][:, :], lhsT=mT[:, k, :],
                                                                    rhs=W_out[:, k, n * 512:(n + 1) * 512],
                                                                    start=(k == 0), stop=(k == 7)),
                          reads=["mT", "Wdone"], writes=[("F", bk)])
                S.add("dve", lambda e, n=n, bk=bk: e.tensor_tensor(
                    out=x[:, n * 512:(n + 1) * 512], in0=x[:, n * 512:(n + 1) * 512], in1=Fb[bk][:, :],
                    op=ALU.add), reads=[xkey, ("F", bk)], writes=[xkey])
                yield
            if raw_out is not None:
                S.add("sp", lambda e: e.dma_start(out=raw_out[rows, :], in_=x[:]),
                      reads=[xkey], writes=[(raw_name, i)], dma=True)
            if norm_out is not None:
                S.add("act", lambda e: e.activation(out=bb[:, 0:512], in_=x[:, 0:512], func=AF.Square,
                                                    accum_out=ss[:, 1:2]),
                      reads=[xkey], writes=["bb", "ss1"])
                S.add("act", lambda e: e.activation(out=bb[:, 0:512], in_=x[:, 512:1024], func=AF.Square,
                                                    accum_out=ss[:, 2:3]),
                      reads=[xkey], writes=["bb", "ss2"])
                S.add("pool", lambda e: e.tensor_tensor(out=ss[:, 1:2], in0=ss[:, 1:2], in1=ss[:, 2:3], op=ALU.add),
                      reads=["ss1", "ss2"], writes=["ss1"])
                S.add("pool", lambda e: e.tensor_scalar(out=rstd[:, 1:2], in0=ss[:, 1:2], scalar1=1.0 / D, scalar2=EPS,
                                                         op0=ALU.mult, op1=ALU.add), reads=["ss1"], writes=["rstd1a"])
                S.add("pool", lambda e: e.tensor_tensor(out=rstd[:, 1:2], in0=rstd[:, 1:2], in1=mhalf[:, 0:1],
                                                         op=ALU.pow), reads=["rstd1a", "mhalf"], writes=["rstd1"])
                S.add("dve", lambda e: e.scalar_tensor_tensor(
                    out=t1[:], in0=x[:], scalar=rstd[:, 1:2], in1=fnw[:], op0=ALU.mult, op1=ALU.mult),
                    reads=[xkey, "rstd1", "fnw"], writes=["E1", "E2"])
                S.add("sp", lambda e: e.dma_start(out=norm_out[rows, :], in_=t1[:]),
                      reads=["E1", "E2"], writes=[(norm_name, i)], dma=True)
            yield

        CUTA = int(os.environ.get("K_CUTA", "99"))
        CUTB = int(os.environ.get("K_CUTB", "99"))

        def limited(g, n, i=None):
            k = 0
            for _ in g:
                k += 1
                if k >= n:
                    break
                yield
            if i is not None and n < 99:
                x = xs[i % 2]
                rows = slice(i * 128, (i + 1) * 128)
                for nm_, o_ in ((raw_name, raw_out), (norm_name, norm_out)):
                    if o_ is not None:
                        S.add("sp", lambda e, o_=o_: e.dma_start(out=o_[rows, :], in_=x[:]),
                              reads=[("xs", i % 2)], writes=[(nm_, i)], dma=True)
            g.close()

        def drain(*gens):
            gens = [g for g in gens if g is not None]
            while gens:
                for g in list(gens):
                    try:
                        next(g)
                    except StopIteration:
                        gens.remove(g)

        if STAGE < 1:
            for i in range(NT):
                do_tile_pt(i)
        elif os.environ.get("K_PIPE", "1") == "1":
            class G:
                def __init__(self, g):
                    self.g, self.n, self.done = g, 0, False

                def step(self):
                    if self.done:
                        return False
                    try:
                        next(self.g)
                        self.n += 1
                        return True
                    except StopIteration:
                        self.done = True
                        return False

            HOLD = int(os.environ.get("K_HOLD", "4"))
            ASTEPS = 12

            def rounds(prims, lead, sec, sec_limit):
                while any(not p.done for p in prims) or (lead is not None and not lead.done):
                    if lead is not None and not lead.done:
                        lead.step()
                    for p in prims:
                        p.step()
                    if sec is not None and (lead is None or lead.done) and sec.n < sec_limit:
                        sec.step()

            a0 = G(stageA(0))
            while a0.step():
                pass
            lead = None
            for i in range(NT):
                sec = G(stageA(i + 1)) if i + 1 < NT else None
                rounds([G(stageATT(i)), G(stageHG(i))], lead, sec, ASTEPS - HOLD)
                rounds([G(stageC(i))], None, sec, ASTEPS - HOLD)
                if sec is not None:
                    while sec.n < ASTEPS - HOLD and sec.step():
                        pass
                lead = sec
            if lead is not None:
                while lead.step():
                    pass
        else:
            for i in range(NT):
                drain(stageA(i))
                drain(stageBC(i))

    for pi_, P in enumerate(passes):
        do_pass(pi_, P)

    fin_reads = []
    for P in passes:
        for k in ("raw_out", "norm_out"):
            nm = P.get(k)
            if nm and nm != "x1":
                fin_reads += [(nm, i) for i in range(NT)]
    fin_reads += [("dbg", nm) for nm in dbg_d]
    S.add("sp", lambda e: e.nop(), reads=fin_reads, writes=["__end"])

    S.ops = [o for o in S.ops if o is not None]
    S.finalize()
    esem = {}
    for en in ("pe", "act", "dve", "pool", "sp"):
        esem[en] = es.enter_context(nc.semaphore("sem_" + en))
    dsem = [es.enter_context(nc.semaphore("dsem%d" % k)) for k in range(S.nd)]
    with nc.Block() as block:
        @block.tensor
        def _(pe):
            S.emit("pe", pe, esem, dsem)

        @block.scalar
        def _(act):
            S.emit("act", act, esem, dsem)

        @block.vector
        def _(dve):
            S.emit("dve", dve, esem, dsem)

        @block.gpsimd
        def _(pool):
            S.emit("pool", pool, esem, dsem)

        @block.sync
        def _(sp):
            S.emit("sp", sp, esem, dsem)
    es.close()
    return nc


def _pk(w, kc):
    n = w.shape[1]
    return np.ascontiguousarray(w.reshape(kc, 128, n).transpose(1, 0, 2).reshape(128, kc * n))


def _layer_inputs(L, sfx, norm_w, w_in, attn_sinks, hgrn_norm_w, w_up_attn, w_up_hgrn, w_out):
    small = np.zeros((128, 19), np.float32)
    small[:, 0:8] = norm_w[L].reshape(8, 128).T
    small[:, 8] = hgrn_norm_w[L]
    small[:, 9:17] = attn_sinks[L][None, :]
    small[:, 17 + L] = 1.0 if L > 0 else 0.0
    return {
        "w_in" + sfx: _pk(w_in[L], 8),
        "w_ua" + sfx: _pk(w_up_attn[L], 4),
        "w_uh" + sfx: _pk(w_up_hgrn[L], 4),
        "w_out" + sfx: _pk(w_out[L], 8),
        "small" + sfx: small,
    }


_NC_CACHE = {}


def _get_nc(key, NT, passes):
    if key not in _NC_CACHE:
        _NC_CACHE[key] = build_nc(NT, passes)
    return _NC_CACHE[key]


def kernel(x, positions, norm_w, w_in, attn_sinks, hgrn_norm_w, w_up_attn, w_up_hgrn, w_out, lb_logits,
           final_norm_w):
    x = np.asarray(x, np.float32)
    positions = np.asarray(positions, np.int32)
    args = [np.asarray(a, np.float32) for a in (norm_w, w_in, attn_sinks, hgrn_norm_w, w_up_attn, w_up_hgrn, w_out)]
    lb_logits = np.asarray(lb_logits, np.float32)
    final_norm_w = np.asarray(final_norm_w, np.float32)
    NT = SEQ // 128
    consts = _consts()
    lbl = np.ascontiguousarray(np.broadcast_to(lb_logits.reshape(1, 2 * 512), (128, 1024)))
    fnw = np.ascontiguousarray(np.broadcast_to(final_norm_w[None, :], (128, D)))
    n_cores = 8
    common = [dict(consts=consts, lbl=lbl, fnw=fnw,
                   pos=np.ascontiguousarray(positions[c % BATCH].reshape(NT, 128).T)) for c in range(n_cores)]
    passes = [dict(src="xin", wsuf="0", raw_out="x1", norm_out=None),
              dict(src="x1", wsuf="1", raw_out=None, norm_out="yout")]
    nc = _get_nc("fused2", NT, passes)
    lw = {}
    for L in range(DEPTH):
        lw.update(_layer_inputs(L, str(L), *args))
    in_maps = []
    for c in range(n_cores):
        m = dict(common[c])
        m.update(lw)
        m["xin"] = np.ascontiguousarray(x[c % BATCH])
        in_maps.append(m)
    res = run_bass_kernel_spmd(nc, in_maps, core_ids=list(range(n_cores)))
    out = np.stack([np.asarray(res.results[b]["yout"]) for b in range(BATCH)], axis=0)
    return out.astype(np.float32)
```
